# Optimizing a Trainium2 kernel written in Bass

```python
import jax, jax.numpy as jnp
from jax import lax
import numpy as np

D_MODEL = 2048
BATCH = 2
SEQ = 4096
DEPTH = 4
DEC_BATCH = 32
DEC_SEQ = 8
PAST_LEN = 16384
PAGE_SIZE = 128

HEAD_DIM = 64
N_A_LAYERS = DEPTH // 2
N_B_LAYERS = DEPTH - N_A_LAYERS
A_HEADS = D_MODEL // HEAD_DIM
A_KV_HEADS = A_HEADS // 8
A_WINDOW = 128
B_HEADS = D_MODEL // HEAD_DIM
B_KV_HEADS = B_HEADS // 8
B_GROUPS = ((128, 1), (512, 4), (2048, 16))
B_N_GROUPS = len(B_GROUPS)
B_MAX_WINDOW = max(w for w, _ in B_GROUPS)
D_FF = 4 * D_MODEL
ROPE_THETA = 10000.0
LN_EPS = 1e-5
BLOCK = 128
DEEPNORM_ALPHA = (2 * DEPTH) ** 0.25
DEEPNORM_BETA = (8 * DEPTH) ** -0.25
NEG_INF = -1e30

kernel_name = 'dilated_yoco_hybrid_step'


def _layernorm(x, g, b):
    x32 = x.astype(jnp.float32)
    mu = x32.mean(-1, keepdims=True)
    var = jnp.square(x32 - mu).mean(-1, keepdims=True)
    y = (x32 - mu) * lax.rsqrt(var + LN_EPS) * g.astype(jnp.float32) + b.astype(jnp.float32)
    return y.astype(x.dtype)


def _rope(x, pos):
    half = HEAD_DIM // 2
    inv = ROPE_THETA ** (-jnp.arange(half, dtype=jnp.float32) / half)
    ang = pos.astype(jnp.float32)[:, None] * inv[None, :]
    cos = jnp.cos(ang)[None, :, None, :]
    sin = jnp.sin(ang)[None, :, None, :]
    x32 = x.astype(jnp.float32)
    x1, x2 = x32[..., :half], x32[..., half:]
    return jnp.concatenate([x1 * cos - x2 * sin, x2 * cos + x1 * sin], axis=-1).astype(x.dtype)


def _probs(s, sink=None):
    m = jnp.max(s, axis=-1, keepdims=True)
    if sink is not None:
        m = jnp.maximum(m, sink)
    e = jnp.exp(s - m)
    den = jnp.sum(e, axis=-1, keepdims=True)
    if sink is not None:
        den = den + jnp.exp(sink - m)
    return e / den, (jnp.log(den) + m)[..., 0]


def _banded_attn(q, k, v, n_back, sink=None):
    n, length, h, hd = q.shape
    kvh = k.shape[2]
    g = h // kvh
    nb = -(-length // BLOCK)
    pad = ((0, 0), (0, nb * BLOCK - length), (0, 0), (0, 0))
    qb = jnp.pad(q, pad).reshape(n, nb, BLOCK, kvh, g, hd)
    kb = jnp.pad(k, pad).reshape(n, nb, BLOCK, kvh, hd)
    vb = jnp.pad(v, pad).reshape(n, nb, BLOCK, kvh, hd)
    prev = lambda t: jnp.pad(t, ((0, 0), (1, 0), (0, 0), (0, 0), (0, 0)))[:, :-1]
    kk = jnp.concatenate([prev(kb), kb], axis=2)
    vv = jnp.concatenate([prev(vb), vb], axis=2)
    s = jnp.einsum('bnqhgd,bnkhd->bnhgqk', qb, kk, preferred_element_type=jnp.float32) * hd ** -0.5
    qi = jnp.arange(BLOCK)[:, None]
    ki = jnp.arange(2 * BLOCK)[None, :]
    dist = BLOCK + qi - ki
    kpos = jnp.arange(nb)[:, None, None] * BLOCK + ki[None] - BLOCK
    valid = (dist >= 0) & (dist <= n_back) & (kpos >= 0)
    s = jnp.where(valid[None, :, None, None], s, NEG_INF)
    if sink is not None:
        sink = sink.astype(jnp.float32).reshape(kvh, g)[None, None, :, :, None, None]
    p, lse = _probs(s, sink)
    o = jnp.einsum('bnhgqk,bnkhd->bnqhgd', p.astype(v.dtype), vv)
    o = o.reshape(n, nb * BLOCK, h, hd)[:, :length]
    lse = lse.transpose(0, 1, 4, 2, 3).reshape(n, nb * BLOCK, h)[:, :length]
    return o, lse


def _window_sample(q, kc, vc, sink):
    n, t, h, hd = q.shape
    kvh = kc.shape[2]
    g = h // kvh
    buf = kc.shape[1] - t
    s = jnp.einsum('bthgd,bkhd->bhgtk', q.reshape(n, t, kvh, g, hd), kc,
                   preferred_element_type=jnp.float32) * hd ** -0.5
    dist = (buf + jnp.arange(t))[:, None] - jnp.arange(buf + t)[None, :]
    valid = (dist >= 0) & (dist < A_WINDOW)
    s = jnp.where(valid, s, NEG_INF)
    p, _ = _probs(s, sink.astype(jnp.float32).reshape(kvh, g)[None, :, :, None, None])
    return jnp.einsum('bhgtk,bkhd->bthgd', p.astype(vc.dtype), vc).reshape(n, t, h, hd)


def _by_residue(x, d):
    n, s = x.shape[:2]
    rest = x.shape[2:]
    return jnp.moveaxis(x.reshape(n, s // d, d, *rest), 2, 1).reshape(n * d, s // d, *rest)


def _from_residue(x, d, n):
    ls = x.shape[1]
    rest = x.shape[2:]
    return jnp.moveaxis(x.reshape(n, d, ls, *rest), 1, 2).reshape(n, d * ls, *rest)


def _combine_groups(outs, lses):
    w = jax.nn.softmax(jnp.stack(lses, axis=0), axis=0)
    return jnp.einsum('gbth,gbthd->bthd', w.astype(outs[0].dtype), jnp.stack(outs, axis=0))


def _dilated_prompt(q, k, v):
    n = q.shape[0]
    outs, lses = [], []
    for gi, (window, d) in enumerate(B_GROUPS):
        o, lse = _banded_attn(_by_residue(q[:, :, gi], d), _by_residue(k, d), _by_residue(v, d), window // d)
        outs.append(_from_residue(o, d, n))
        lses.append(_from_residue(lse, d, n))
    return _combine_groups(outs, lses)


def _dilated_sample(q, kc, vc):
    n, t, _, h, hd = q.shape
    kvh = kc.shape[2]
    g = h // kvh
    buf = kc.shape[1] - t
    outs, lses = [], []
    for gi, (window, d) in enumerate(B_GROUPS):
        n_keys = window // d + 1
        idx = buf + jnp.arange(t)[:, None] - jnp.arange(n_keys)[None, :] * d
        valid = idx >= 0
        idx = jnp.maximum(idx, 0)
        kg = kc[:, idx]
        vg = vc[:, idx]
        s = jnp.einsum('bthgd,btjhd->bhgtj', q[:, :, gi].reshape(n, t, kvh, g, hd), kg,
                       preferred_element_type=jnp.float32) * hd ** -0.5
        s = jnp.where(valid, s, NEG_INF)
        p, lse = _probs(s)
        outs.append(jnp.einsum('bhgtj,btjhd->bthgd', p.astype(vg.dtype), vg).reshape(n, t, h, hd))
        lses.append(lse.transpose(0, 3, 1, 2).reshape(n, t, h))
    return _combine_groups(outs, lses)


def _trunk(x, pos, attend_a, prepare_b, attend_b, ln_g, ln_b, w_qkv_a, sinks_a, w_o_a,
           w_kv_b, w_q_b, w_o_b, w_up, w_down):
    n, t, _ = x.shape
    qa = A_HEADS * HEAD_DIM
    kva = A_KV_HEADS * HEAD_DIM
    kvb = B_KV_HEADS * HEAD_DIM
    a_states = []
    b_ctx, b_state = None, None
    for layer in range(DEPTH):
        if layer < N_A_LAYERS:
            qkv = x @ w_qkv_a[layer]
            q = _rope(qkv[..., :qa].reshape(n, t, A_HEADS, HEAD_DIM), pos)
            k = _rope(qkv[..., qa:qa + kva].reshape(n, t, A_KV_HEADS, HEAD_DIM), pos)
            v = qkv[..., qa + kva:].reshape(n, t, A_KV_HEADS, HEAD_DIM)
            o, state = attend_a(layer, q, k, v, sinks_a[layer])
            a_states.append(state)
            mix = o.reshape(n, t, qa) @ w_o_a[layer]
        else:
            if layer == N_A_LAYERS:
                kv = x @ w_kv_b
                kb = _rope(kv[..., :kvb].reshape(n, t, B_KV_HEADS, HEAD_DIM), pos)
                vb = kv[..., kvb:].reshape(n, t, B_KV_HEADS, HEAD_DIM)
                b_ctx, b_state = prepare_b(kb, vb)
            j = layer - N_A_LAYERS
            q = _rope((x @ w_q_b[j]).reshape(n, t, B_N_GROUPS * B_HEADS, HEAD_DIM), pos)
            o = attend_b(q.reshape(n, t, B_N_GROUPS, B_HEADS, HEAD_DIM), *b_ctx)
            mix = o.reshape(n, t, B_HEADS * HEAD_DIM) @ w_o_b[j]
        x = _layernorm(DEEPNORM_ALPHA * x + mix, ln_g[layer, 0], ln_b[layer, 0])
        hid = jnp.square(jax.nn.relu(x @ w_up[layer]))
        x = _layernorm(DEEPNORM_ALPHA * x + hid @ w_down[layer], ln_g[layer, 1], ln_b[layer, 1])
    return x, a_states, b_state


def setup_inputs(seed: int = 0) -> dict:
    key = jax.random.key(seed)
    ks = jax.random.split(key, 20)
    f32 = jnp.float32
    nrm = lambda k, shape, scale: jax.random.normal(k, shape, f32) * scale
    a_buf = min(A_WINDOW, PAST_LEN)
    b_buf = min(B_MAX_WINDOW, PAST_LEN)
    return {
        'x_prompt': nrm(ks[0], (BATCH, SEQ, D_MODEL), 1.0),
        'x_sample': nrm(ks[1], (DEC_BATCH, DEC_SEQ, D_MODEL), 1.0),
        'cache_a_k': nrm(ks[2], (N_A_LAYERS, DEC_BATCH, a_buf, A_KV_HEADS, HEAD_DIM), 1.0),
        'cache_a_v': nrm(ks[3], (N_A_LAYERS, DEC_BATCH, a_buf, A_KV_HEADS, HEAD_DIM), 1.0),
        'cache_b_k': nrm(ks[4], (DEC_BATCH, b_buf, B_KV_HEADS, HEAD_DIM), 1.0),
        'cache_b_v': nrm(ks[5], (DEC_BATCH, b_buf, B_KV_HEADS, HEAD_DIM), 1.0),
        'ln_g': 1.0 + nrm(ks[6], (DEPTH, 2, D_MODEL), 0.02),
        'ln_b': nrm(ks[7], (DEPTH, 2, D_MODEL), 0.02),
        'w_qkv_a': nrm(ks[8], (N_A_LAYERS, D_MODEL, (A_HEADS + 2 * A_KV_HEADS) * HEAD_DIM), D_MODEL ** -0.5),
        'sinks_a': nrm(ks[9], (N_A_LAYERS, A_HEADS), 0.5),
        'w_o_a': nrm(ks[10], (N_A_LAYERS, A_HEADS * HEAD_DIM, D_MODEL), DEEPNORM_BETA * (A_HEADS * HEAD_DIM) ** -0.5),
        'w_kv_b': nrm(ks[11], (D_MODEL, 2 * B_KV_HEADS * HEAD_DIM), D_MODEL ** -0.5),
        'w_q_b': nrm(ks[12], (N_B_LAYERS, D_MODEL, B_N_GROUPS * B_HEADS * HEAD_DIM), D_MODEL ** -0.5),
        'w_o_b': nrm(ks[13], (N_B_LAYERS, B_HEADS * HEAD_DIM, D_MODEL), DEEPNORM_BETA * (B_HEADS * HEAD_DIM) ** -0.5),
        'w_up': nrm(ks[14], (DEPTH, D_MODEL, D_FF), D_MODEL ** -0.5),
        'w_down': nrm(ks[15], (DEPTH, D_FF, D_MODEL), DEEPNORM_BETA * D_FF ** -0.5),
    }


def reference(x_prompt, x_sample, cache_a_k, cache_a_v, cache_b_k, cache_b_v, ln_g, ln_b,
              w_qkv_a, sinks_a, w_o_a, w_kv_b, w_q_b, w_o_b, w_up, w_down):
    weights = (ln_g, ln_b, w_qkv_a, sinks_a, w_o_a, w_kv_b, w_q_b, w_o_b, w_up, w_down)

    def prompt_a(layer, q, k, v, sink):
        o, _ = _banded_attn(q, k, v, A_WINDOW - 1, sink)
        keep = min(A_WINDOW, k.shape[1])
        return o, (k[:, -keep:], v[:, -keep:])

    def sample_a(layer, q, k, v, sink):
        kc = jnp.concatenate([cache_a_k[layer], k], axis=1)
        vc = jnp.concatenate([cache_a_v[layer], v], axis=1)
        keep = cache_a_k.shape[2]
        return _window_sample(q, kc, vc, sink), (kc[:, -keep:], vc[:, -keep:])

    def prompt_b(k, v):
        keep = min(B_MAX_WINDOW, k.shape[1])
        return (k, v), (k[:, -keep:], v[:, -keep:])

    def sample_b(k, v):
        kc = jnp.concatenate([cache_b_k, k], axis=1)
        vc = jnp.concatenate([cache_b_v, v], axis=1)
        keep = cache_b_k.shape[1]
        return (kc, vc), (kc[:, -keep:], vc[:, -keep:])

    pos_prompt = jnp.arange(x_prompt.shape[1], dtype=jnp.int32)
    pos_sample = PAST_LEN + jnp.arange(x_sample.shape[1], dtype=jnp.int32)
    y_prompt, a_prompt, b_prompt = _trunk(x_prompt, pos_prompt, prompt_a, prompt_b, _dilated_prompt, *weights)
    y_sample, a_sample, b_sample = _trunk(x_sample, pos_sample, sample_a, sample_b, _dilated_sample, *weights)
    new_a_k_prompt = jnp.stack([s[0] for s in a_prompt], axis=0)
    new_a_v_prompt = jnp.stack([s[1] for s in a_prompt], axis=0)
    new_a_k_sample = jnp.stack([s[0] for s in a_sample], axis=0)
    new_a_v_sample = jnp.stack([s[1] for s in a_sample], axis=0)
    new_b_k_prompt, new_b_v_prompt = b_prompt
    new_b_k_sample, new_b_v_sample = b_sample
    return (y_prompt, y_sample, new_a_k_prompt, new_a_v_prompt, new_a_k_sample, new_a_v_sample,
            new_b_k_prompt, new_b_v_prompt, new_b_k_sample, new_b_v_sample)
```

```python
import numpy as np
import ml_dtypes
from contextlib import ExitStack
import concourse.bass as bass
import concourse.mybir as mybir
from concourse.bass_utils import run_bass_kernel_spmd

F32 = mybir.dt.float32
BF16 = mybir.dt.bfloat16
ALU = mybir.AluOpType
AF = mybir.ActivationFunctionType

D = 2048
NCH = 16
TP = 1024
TS = 32
T = TP + TS
DFF = 8192
HD = 64
NCORE = 8
PAST = 16384
ALPHA = 8.0 ** 0.25
EPS = 1e-5
TTS = [(0, 512), (512, 512), (1024, 32)]
NSLOT = 4
PW = 256
NDMASEM = 24
SEMCH = 1000


class Op:
    __slots__ = ("eng", "fn", "dma", "deps", "idx", "ms", "sem", "val", "waits", "coll")

    def __init__(self, eng, fn, dma):
        self.eng = eng
        self.fn = fn
        self.dma = dma
        self.deps = set()
        self.ms = None
        self.sem = None
        self.val = None
        self.waits = []
        self.coll = False


class Tracker:
    def __init__(self):
        self.ops = []
        self.bufs = {}
        self.dma_last = {}
        self.dma_count = {}
        self.ndma = 0
        self.ndma_q = {}
        self.ncoll = 0

    def add(self, eng, fn, reads=(), writes=(), dma=False, coll=False):
        op = Op(eng, fn, dma or coll)
        op.coll = coll
        op.idx = len(self.ops)
        for (b, lo, hi) in reads:
            ents = self.bufs.setdefault(b, [])
            for e in ents:
                if e[0] < hi and lo < e[1]:
                    if e[2] is not None:
                        op.deps.add(e[2])
                    if not op.dma:
                        e[3][:] = [r for r in e[3] if r.dma or r.eng != op.eng]
                    e[3].append(op)
        for (b, lo, hi) in writes:
            ents = self.bufs.setdefault(b, [])
            keep = []
            for e in ents:
                if e[0] < hi and lo < e[1]:
                    if e[2] is not None:
                        op.deps.add(e[2])
                    for r in e[3]:
                        op.deps.add(r)
                    if lo <= e[0] and e[1] <= hi:
                        continue
                keep.append(e)
            keep.append([lo, hi, op, []])
            self.bufs[b] = keep
        op.deps.discard(op)
        if coll:
            op.sem = ("coll", self.ncoll)
            op.val = 1
            self.ncoll += 1
        elif dma:
            k = self.ndma_q.get(eng, 0)
            self.ndma_q[eng] = k + 1
            s = (eng, k % NDMASEM)
            self.ndma += 1
            prev = self.dma_last.get(s)
            if prev is not None:
                op.deps.add(prev)
            self.dma_last[s] = op
            self.dma_count[s] = self.dma_count.get(s, 0) + 1
            op.sem = ("dma", s)
            op.val = 16 * self.dma_count[s]
        self.ops.append(op)
        return op

    def finalize(self):
        def needs(d, op):
            return (not d.dma) and (d.eng != op.eng or op.eng != "pe")
        for op in self.ops:
            for d in op.deps:
                if needs(d, op):
                    d.ms = 0
        cnt = {}
        for op in self.ops:
            if op.ms is not None:
                cnt[op.eng] = cnt.get(op.eng, 0) + 1
                op.ms = cnt[op.eng]
        waited = {}
        for op in self.ops:
            need = {}
            for d in op.deps:
                if d.dma:
                    key = d.sem
                    v = d.val
                elif needs(d, op):
                    key = ("eng", d.eng)
                    v = d.ms
                else:
                    continue
                if v > need.get(key, 0):
                    need[key] = v
            w = waited.setdefault(op.eng, {})
            for key, v in need.items():
                if v > w.get(key, 0):
                    w[key] = v
                    if key[0] == "eng":
                        op.waits.append((("eng", key[1], (v - 1) // SEMCH), (v - 1) % SEMCH + 1))
                    else:
                        op.waits.append((key, v))


class Region:
    def __init__(self, buf, handle, handle_dt_size, byte_off, dt, shape):
        self.buf = buf
        self.h = handle
        self.hs = handle_dt_size
        self.off = byte_off
        self.dt = dt
        self.es = 2 if dt == BF16 else 4
        self.shape = tuple(shape)
        n = 1
        for s in shape:
            n *= s
        self.n = n
        base = handle[:, byte_off // handle_dt_size:(byte_off + n * self.es) // handle_dt_size]
        if (dt == BF16) != (handle_dt_size == 2):
            base = base.bitcast(dt)
        if len(shape) == 1:
            self.base = base
        else:
            names = "abcdefg"[:len(shape)]
            kw = {names[i]: shape[i] for i in range(len(shape))}
            self.base = base.rearrange("p (%s) -> p %s" % (" ".join(names), " ".join(names)), **kw)

    def __call__(self, *idx, p=None):
        sl = []
        lo = 0
        hi = 0
        stride = self.n
        for d, ix in enumerate(idx):
            stride //= self.shape[d]
            if isinstance(ix, int):
                sl.append(slice(ix, ix + 1))
                lo += ix * stride
                hi += ix * stride
            else:
                st, cnt = ix[0], ix[1]
                step = ix[2] if len(ix) > 2 else 1
                sl.append(slice(st, st + (cnt - 1) * step + 1, step))
                lo += st * stride
                hi += (st + (cnt - 1) * step) * stride
        for d in range(len(idx), len(self.shape)):
            stride //= self.shape[d]
            sl.append(slice(0, self.shape[d]))
            hi += (self.shape[d] - 1) * stride
        ps = slice(0, 128) if p is None else slice(p[0], p[0] + p[1])
        ap = self.base[(ps,) + tuple(sl)]
        blo, bhi = self.off + lo * self.es, self.off + (hi + 1) * self.es
        if self.buf == "PS":
            blo = (blo // 2048) * 2048
            bhi = ((bhi + 2047) // 2048) * 2048
        return ap, (self.buf, blo, bhi)


def sq(ap, *axes):
    for a in sorted(axes, reverse=True):
        ap = ap.squeeze(a + 1)
    return ap


def qperm_cols(nheads_groups=1):
    cols = []
    for i in range(8):
        for half in range(2):
            for j in range(4):
                h = 8 * j + i
                cols.extend(range(h * 64 + half * 32, h * 64 + half * 32 + 32))
    return np.array(cols)


def kperm_cols():
    cols = []
    for half in range(2):
        for j in range(4):
            cols.extend(range(j * 64 + half * 32, j * 64 + half * 32 + 32))
    return np.array(cols)


def operm_rows():
    rows = []
    for i in range(8):
        for a in range(2):
            for p in range(128):
                h = 8 * (2 * a + p // 64) + i
                rows.append(h * 64 + p % 64)
    return np.array(rows)


def rope_tables(pos):
    half = HD // 2
    inv = (np.float32(10000.0) ** (-np.arange(half, dtype=np.float32) / np.float32(half))).astype(np.float32)
    ang = (pos.astype(np.float32)[:, None] * inv[None, :]).astype(np.float32)
    return np.cos(ang).astype(np.float32), np.sin(ang).astype(np.float32)


class _Stop(Exception):
    pass


def build_program(stage=None, fake_gather=False, small=False):
    nc = bass.Bass(target_bir_lowering=False)

    def checkpoint(n):
        if stage == n:
            raise _Stop()

    tr = Tracker()
    dr = {}

    def din(name, shape, dt=F32):
        dr[name] = nc.dram_tensor(name, list(shape), dt, kind="ExternalInput")
        return dr[name]

    def dout(name, shape, dt=F32):
        dr[name] = nc.dram_tensor(name, list(shape), dt, kind="ExternalOutput")
        return dr[name]

    def dscr(name, shape, dt):
        dr[name] = nc.dram_tensor(name, list(shape), dt)
        return dr[name]

    x_in = din("x_in", [T, D])
    cak = din("cak", [2, 4, 128, 256])
    cav = din("cav", [2, 4, 128, 256])
    cbk = din("cbk", [4, 2048, 256])
    cbv = din("cbv", [4, 2048, 256])
    lng = din("lng", [128, 8 * 16])
    lnb = din("lnb", [128, 8 * 16])
    sinks = din("sinks", [128, 32])
    wqa = din("wqa", [2, D, 2304])
    wkva = din("wkva", [2, D, 512])
    woa = din("woa", [2, D, D])
    if small == 2:
        wkvb = din("wkvb", [D, 768])
        wqb = din("wqb", [2, D, 6144])
        wob = din("wob", [2, D, D])
        wup = din("wup", [4, 128, 256])
        wdn = din("wdn", [4, 128, 256])
    elif small:
        wkvb = din("wkvb", [128, 768])
        wqb = din("wqb", [2, 128, 256])
        wob = din("wob", [2, 128, 256])
        wup = din("wup", [4, 128, 256])
        wdn = din("wdn", [4, 128, 256])
    else:
        wkvb = din("wkvb", [D, 768])
        wqb = din("wqb", [2, D, 6144])
        wob = din("wob", [2, D, D])
        wup = din("wup", [4, D, DFF])
        wdn = din("wdn", [4, DFF, D])
    cosf = din("cosf", [128, T])
    sinf = din("sinf", [128, T])
    cost = din("cost", [128, 9 * 32])
    sint = din("sint", [128, 9 * 32])
    masks = din("masks", [128, 8 * 128], BF16)
    identb = din("identb", [128, 128], BF16)
    identf = din("identf", [128, 128])

    y = dout("y", [T, D])
    oak_p = dout("oak_p", [2, 128, 256])
    oav_p = dout("oav_p", [2, 128, 256])
    oak_s = dout("oak_s", [2, 4, 128, 256])
    oav_s = dout("oav_s", [2, 4, 128, 256])
    obk_p = dout("obk_p", [TP, 256])
    obv_p = dout("obv_p", [TP, 256])
    obk_s = dout("obk_s", [4, 2048, 256])
    obv_s = dout("obv_s", [4, 2048, 256])

    gin_a = [dscr("gin_a%d" % l, [128, 512], BF16) for l in range(2)]
    gout_a = [dscr("gout_a%d" % l, [NCORE * 128, 512], BF16) for l in range(2)]
    gin_bk = dscr("gin_bk", [128, 2048], BF16)
    gout_bk = dscr("gout_bk", [NCORE * 128, 2048], BF16)
    gin_bv = dscr("gin_bv", [TP, 256], BF16)
    gout_bv = dscr("gout_bv", [NCORE * TP, 256], BF16)
    kbs_scr = dscr("kbs_scr", [128, 64], BF16)
    vbs_scr = dscr("vbs_scr", [8, 4 * 256], BF16)

    es = ExitStack()

    def sb(name, shape, dt):
        return es.enter_context(nc.sbuf_tensor(name, list(shape), dt))

    XBh = sb("XB", [128, NCH * T], BF16)
    ZRh = sb("ZR", [128, NCH * T], F32)
    U1h = sb("U1", [128, NCH * T], BF16)
    WRh = sb("WR", [128, NSLOT * 16 * PW], BF16)
    CSh = sb("CS", [128, 2 * T], F32)
    CTh = sb("CT", [128, 2 * 9 * 32], F32)
    MKh = sb("MK", [128, 8 * 128], BF16)
    LNh = sb("LNP", [128, 2 * 128], F32)
    SKh = sb("SK", [128, 32], F32)
    IDBh = sb("IDB", [128, 128], BF16)
    IDFh = sb("IDF", [128, 128], F32)
    ONEh = sb("ONE", [128, 128], BF16)
    TMh = sb("TM", [128, 6 * 512], F32)
    TBh = sb("TB", [128, 6 * 512], BF16)
    STh = sb("STG", [128, 2 * 512], F32)
    RDh = sb("RD", [128, 256], F32)
    PSh = es.enter_context(nc.psum_tensor("PS", [128, 8 * 512], F32))

    XB = Region("XB", XBh, 2, 0, BF16, (NCH, T))
    Z = Region("ZR", ZRh, 4, 0, F32, (NCH, T))
    OT = Region("U1", U1h, 2, 0, BF16, (NCH, T))
    HID = OT
    XSTG = Region("U1", U1h, 2, 0, BF16, (2, D))
    OSTG = Region("U1", U1h, 2, 0, F32, (2, D))
    WR = Region("WR", WRh, 2, 0, BF16, (NSLOT, 16, PW))
    COSF = Region("CS", CSh, 4, 0, F32, (T,))
    SINF = Region("CS", CSh, 4, T * 4, F32, (T,))
    COST = Region("CT", CTh, 4, 0, F32, (9, 32))
    SINT = Region("CT", CTh, 4, 9 * 32 * 4, F32, (9, 32))
    MK = Region("MK", MKh, 2, 0, BF16, (8, 128))
    LNG = Region("LNP", LNh, 4, 0, F32, (8, 16))
    LNB = Region("LNP", LNh, 4, 128 * 4, F32, (8, 16))
    SK = Region("SK", SKh, 4, 0, F32, (2, 16))
    IDB = Region("IDB", IDBh, 2, 0, BF16, (128,))
    IDF = Region("IDF", IDFh, 4, 0, F32, (128,))
    ONE = Region("ONE", ONEh, 2, 0, BF16, (128,))
    TM = Region("TM", TMh, 4, 0, F32, (6, 512))
    TB = Region("TB", TBh, 2, 0, BF16, (6, 512))
    STG = Region("STG", STh, 4, 0, F32, (2, 512))
    RD = Region("RD", RDh, 4, 0, F32, (256,))
    PS = Region("PS", PSh, 4, 0, F32, (8, 512))
    PSB = Region("PS", PSh, 4, 0, BF16, (8, 1024))

    def zr(off_f32, dt, shape):
        return Region("ZR", ZRh, 4, off_f32 * 4, dt, shape)

    QT = zr(0, BF16, (16, T))
    KT = zr(8448, BF16, (2, 1184))
    VT = zr(9632, BF16, (9, 256))
    VS = zr(10784, BF16, (4, 256))
    KCT = zr(11296, BF16, (4, 2, 128))
    VC = zr(11808, BF16, (4, 256))
    KTOKC = zr(12320, BF16, (4, 256))
    ACC = zr(0, F32, (2, 2, TP))
    KTB = zr(4096, BF16, (2, 3104))
    VB1 = zr(7200, BF16, (9, 256))
    VB4 = zr(8352, BF16, (4, 3, 256))
    VB16 = zr(9888, BF16, (16, 2, 256))
    QBUF = zr(13984, BF16, (2, 2, T))
    QS = zr(16096, BF16, (3, 8, 2, TS))
    KTOKB = zr(8352, BF16, (13, 256))
    KCTB = zr(8352 + 1664, BF16, (13, 2, 128))
    VCB = zr(8352 + 3328, BF16, (13, 256))
    VSB = zr(7200, BF16, (4, 256))

    def E(eng, fn, reads=(), writes=(), dma=False, coll=False):
        return tr.add(eng, fn, reads, writes, dma, coll)

    def dram_res(name, lo=0, hi=1 << 40):
        return (name, lo, hi)

    psum_state = {}

    def ps_meta(ap):
        esz = 4 if ap.dtype == F32 else 2
        stride0, npart = ap.ap[0]
        p0 = ap.offset // stride0
        c0 = ap.offset % stride0
        ext = 0
        for (st_, cn_) in ap.ap[1:]:
            ext += (cn_ - 1) * abs(st_)
        b0 = c0 * esz
        b1 = (c0 + ext + 1) * esz
        bank = b0 // 2048
        assert (b1 - 1) // 2048 == bank
        quads = tuple(range(p0 // 32, (p0 + npart - 1) // 32 + 1))
        return bank, quads, b0, b1

    def start_flag(ap, first, last):
        bank, quads, b0, b1 = ps_meta(ap)
        key = (b0, b1)
        use_start = False
        if first:
            any_open = any(psum_state.setdefault((bank, q), {"open": set(), "wr": []})["open"] for q in quads)
            if not any_open:
                use_start = True
                for q in quads:
                    st_ = psum_state[(bank, q)]
                    st_["wr"] = []
            else:
                for q in quads:
                    for (w0, w1) in psum_state[(bank, q)]["wr"]:
                        assert not (w0 < b1 and b0 < w1), "psum group start on dirty columns"
            for q in quads:
                st_ = psum_state[(bank, q)]
                st_["open"].add(key)
                st_["wr"].append(key)
        if last:
            for q in quads:
                st_ = psum_state.setdefault((bank, q), {"open": set(), "wr": []})
                st_["open"].discard(key)
        return use_start

    def mm(out, lhsT, rhs, start, stop, tp=None):
        (oa, orr), (la, lr), (ra, rr) = out, lhsT, rhs
        start = start_flag(oa, start, stop)
        if tp is None:
            E("pe", lambda e: e.matmul(oa, la, ra, start=start, stop=stop, skip_group_check=True), [lr, rr], [orr])
        else:
            E("pe", lambda e: e.matmul(oa, la, ra, start=start, stop=stop, tile_position=tp, skip_group_check=True),
              [lr, rr], [orr])

    def transpose(out, in_, ident):
        (oa, orr), (ia, ir), (da, drr) = out, in_, ident
        E("pe", lambda e: e.transpose(oa, ia, da), [ir, drr], [orr])

    def act(out, in_, func, bias=None, scale=None):
        (oa, orr), (ia, ir) = out, in_
        reads = [ir]
        kw = {}
        if bias is not None:
            if isinstance(bias, tuple):
                kw["bias"] = bias[0]
                reads.append(bias[1])
            else:
                kw["bias"] = bias
        if scale is not None:
            if isinstance(scale, tuple):
                kw["scale"] = scale[0]
                reads.append(scale[1])
            else:
                kw["scale"] = scale
        E("act", lambda e: e.activation(oa, ia, func, **kw), reads, [orr])

    def tt(eng, out, a, b, op):
        (oa, orr), (aa, ar), (ba, br) = out, a, b
        E(eng, lambda e: e.tensor_tensor(oa, aa, ba, op), [ar, br], [orr])

    def ts(eng, out, a, s1, op0, s2=None, op1=None):
        (oa, orr), (aa, ar) = out, a
        reads = [ar]
        if isinstance(s1, tuple):
            reads.append(s1[1])
            s1 = s1[0]
        if op1 is None:
            E(eng, lambda e: e.tensor_scalar(oa, aa, s1, None, op0), reads, [orr])
        else:
            E(eng, lambda e: e.tensor_scalar(oa, aa, s1, s2, op0, op1), reads, [orr])

    def stt(eng, out, a, scalar, b, op0, op1):
        (oa, orr), (aa, ar), (ba, br) = out, a, b
        E(eng, lambda e: e.scalar_tensor_tensor(oa, aa, scalar, ba, op0, op1), [ar, br], [orr])

    def cp(eng, out, in_):
        (oa, orr), (ia, ir) = out, in_
        if eng == "act":
            E("act", lambda e: e.activation(oa, ia, AF.Copy), [ir], [orr])
        else:
            E(eng, lambda e: e.tensor_copy(oa, ia), [ir], [orr])

    def recip(out, in_):
        (oa, orr), (ia, ir) = out, in_
        E("dve", lambda e: e.reciprocal(oa, ia), [ir], [orr])

    def dma(q, out, in_, reads, writes):
        E(q, lambda e: e.dma_start(out=out, in_=in_), reads, writes, dma=True)

    def gather(g_in, g_out, rows):
        if fake_gather:
            for r in range(NCORE):
                dma("pool", g_out[r * rows:(r + 1) * rows, :], g_in[:, :], [dram_res(g_in.name)], [dram_res(g_out.name)])
        else:
            op = E("pool", lambda e: e.collective_compute("AllGather", ALU.bypass, replica_groups=[list(range(NCORE))],
                                                          ins=[g_in.ap().opt()], outs=[g_out.ap().opt()]),
                   [dram_res(g_in.name)], [dram_res(g_out.name)], coll=True)
            for prev in tr.dma_last.values():
                if prev is not op:
                    op.deps.add(prev)

    dma("sp", CSh[:, 0:T], cosf[:, :], [], [COSF()[1]])
    dma("sp", CSh[:, T:2 * T], sinf[:, :], [], [SINF()[1]])
    dma("sp", CTh[:, 0:288], cost[:, :], [], [COST()[1]])
    dma("sp", CTh[:, 288:576], sint[:, :], [], [SINT()[1]])
    dma("sp", MKh[:, :], masks[:, :], [], [MK()[1]])
    dma("sp", LNh[:, 0:128], lng[:, :], [], [LNG()[1]])
    dma("sp", LNh[:, 128:256], lnb[:, :], [], [LNB()[1]])
    dma("sp", SKh[:, :], sinks[:, :], [], [SK()[1]])
    dma("sp", IDBh[:, :], identb[:, :], [], [IDB()[1]])
    dma("sp", IDFh[:, :], identf[:, :], [], [IDF()[1]])
    E("dve", lambda e: e.memset(ONEh[:, :], 1.0), [], [ONE()[1]])
    act(SK(), SK(), AF.Exp)

    panels = []

    def wview(h, lead, k0, c0, ncols=PW):
        if lead is not None:
            ap = h[lead, k0:k0 + D, c0:c0 + ncols]
        else:
            ap = h[k0:k0 + D, c0:c0 + ncols]
        return ap.rearrange("(kc p) n -> p kc n", p=128)

    for l in range(2):
        for i in range(9):
            panels.append(wview(wqa, l, 0, i * PW))
        panels.append(wview(wkva, l, 0, 0))
        panels.append(wview(wkva, l, 0, 256))
        for i in range(8):
            panels.append(wview(woa, l, 0, i * PW))
        if small == 1:
            break
        if small == 2:
            continue
        for hg in range(4):
            for i in range(8):
                panels.append(wview(wup, l, 0, hg * D + i * PW))
            for i in range(8):
                panels.append(wview(wdn, l, hg * D, i * PW))
    for l in range(2, 4):
        if small == 1:
            break
        jb = l - 2
        if l == 2:
            for i in range(3):
                panels.append(wview(wkvb, None, 0, i * PW))
        for i in range(8):
            for g in range(3):
                panels.append(wview(wqb, jb, 0, g * D + i * PW))
        for i in range(8):
            panels.append(wview(wob, jb, 0, i * PW))
        if small == 2:
            continue
        for hg in range(4):
            for i in range(8):
                panels.append(wview(wup, l, 0, hg * D + i * PW))
            for i in range(8):
                panels.append(wview(wdn, l, hg * D, i * PW))

    pstate = {"issued": 0, "next": 0}

    def issue_panels(held=0):
        while pstate["issued"] < len(panels) and pstate["issued"] < pstate["next"] + NSLOT - held:
            p = pstate["issued"]
            slot = p % NSLOT
            ap, res = WR(slot)
            src = panels[p]
            dma("pool", sq(ap, 0), src, [], [res])
            pstate["issued"] += 1

    def next_panel(held=0):
        issue_panels(held)
        p = pstate["next"]
        pstate["next"] += 1
        return p % NSLOT

    def Wl(slot, kc, c0, n):
        ap, res = WR(slot, kc, (c0, n))
        return sq(ap, 0, 1), res

    rot = {"i": 0}

    def next_bank(nb=4):
        b = rot["i"] % nb
        rot["i"] += 1
        return b

    def xb_cols(kc, c0, n):
        ap, res = XB(kc, (c0, n))
        return sq(ap, 0), res

    def ps_bank(b, n, p=None, c0=0):
        ap, res = PS(b, (c0, n), p=p)
        return sq(ap, 0), res

    def tm(i, n, p=None):
        ap, res = TM(i, (0, n), p=p)
        return sq(ap, 0), res

    def tb(i, n, p=None, c0=0):
        ap, res = TB(i, (c0, n), p=p)
        return sq(ap, 0), res

    TOKT = [(i * 128, 128) for i in range(8)] + [(1024, 32)]
    for ti, (r0, nt) in enumerate(TOKT):
        sbuf_i = ti % 2
        sap, sres = XSTG(sbuf_i, p=(0, nt))
        dma("pool", sq(sap, 0), x_in[r0:r0 + nt, :], [], [sres])
        for hb in range(2):
            bank = 4 + hb
            for k in range(8):
                kc = hb * 8 + k
                oap, ores = PSB(bank, (k * 128, nt))
                iap, ires = XSTG(sbuf_i, (kc * 128, 128), p=(0, nt))
                dap, dres = IDB((0, nt), p=(0, nt))
                transpose((sq(oap, 0), ores), (sq(iap, 0), ires), (dap, dres))
            oap, ores = PSB(bank, (0, 1024))
            src = sq(oap, 0).rearrange("p (k n) -> p k n", k=8)[:, :, 0:nt]
            dap, dres = XB((hb * 8, 8), (r0, nt))
            cp("act" if hb == 0 else "dve", (dap, dres), (src, ores))

    def rope_pair(bA, bB, c0, n, dst1, dst2):
        A = tm(0, n)
        B = tm(1, n)
        cp("act", A, ps_bank(bA, n))
        cp("act", B, ps_bank(bB, n))
        cs = (COSF((c0, n))[0], COSF((c0, n))[1])
        sn = (SINF((c0, n))[0], SINF((c0, n))[1])
        t1 = tm(2, n)
        t2 = tm(3, n)
        tt("dve", t1, A, cs, ALU.mult)
        tt("dve", t2, B, sn, ALU.mult)
        tt("dve", dst1, t1, t2, ALU.subtract)
        tt("dve", t1, B, cs, ALU.mult)
        tt("dve", t2, A, sn, ALU.mult)
        tt("dve", dst2, t1, t2, ALU.add)

    def proj_pair(slot, dstfn, fixed=False):
        for (c0, n) in TTS:
            bsel = 0 if fixed else next_bank(2)
            bA, bB = 2 * bsel, 2 * bsel + 1
            for oc, bk in ((0, bA), (1, bB)):
                for kc in range(16):
                    mm(ps_bank(bk, n), Wl(slot, kc, oc * 128, 128), xb_cols(kc, c0, n), kc == 0, kc == 15)
            rope_pair(bA, bB, c0, n, dstfn(0, c0, n), dstfn(1, c0, n))

    def rope_tok(psK, nt, tidx, dst):
        ka_, kr_ = TM(4, (0, 256), p=(0, nt))
        ks = (sq(ka_, 0), kr_)
        cp("act", ks, psK)
        cap, cres = COST(tidx, p=(0, nt))
        sap, sres = SINT(tidx, p=(0, nt))
        cs_ = (sq(cap, 0), cres)
        sn_ = (sq(sap, 0), sres)
        dap, dres = dst
        t1a, t1r = TM(2, (0, 32), p=(0, nt))
        t2a, t2r = TM(3, (0, 32), p=(0, nt))
        t1 = (sq(t1a, 0), t1r)
        t2 = (sq(t2a, 0), t2r)
        for h in range(4):
            x1 = (ks[0][:, h * 64:h * 64 + 32], kr_)
            x2 = (ks[0][:, h * 64 + 32:h * 64 + 64], kr_)
            d1 = (dap[:, h * 64:h * 64 + 32], dres)
            d2 = (dap[:, h * 64 + 32:h * 64 + 64], dres)
            tt("dve", t1, x1, cs_, ALU.mult)
            tt("dve", t2, x2, sn_, ALU.mult)
            tt("dve", d1, t1, t2, ALU.subtract)
            tt("dve", t1, x2, cs_, ALU.mult)
            tt("dve", t2, x1, sn_, ALU.mult)
            tt("dve", d2, t1, t2, ALU.add)

    def tok_proj(slot, c0, nt, bank, col0):
        for kc in range(16):
            mm(ps_bank(bank, PW, p=(0, nt), c0=col0), xb_cols(kc, c0, nt), Wl(slot, kc, 0, PW), kc == 0, kc == 15)

    arot = {"st": 0, "ud": 0, "pt": 0}

    apipe = {"pending": None}

    def attn_flush():
        if apipe["pending"] is not None:
            f = apipe["pending"]
            apipe["pending"] = None
            f()

    def attn_kb(nk, W, qfn, ktfn, vfn, mask, nq, npairs, udbank, first, last, ucols=None, post=None):
        for half in range(2):
            for j in range(4):
                oap, ores = PS(2 + j, (0, W), p=(0, nk))
                oap = sq(oap, 0)
                if npairs > 1:
                    oap = oap.rearrange("p (i t) -> p i t", i=npairs)
                qa, qr = qfn(j, half)
                ka, kr = ktfn(j, half)
                mm((oap, ores), (ka, kr), (qa, qr), half == 0, half == 1, tp=(32 * j, 0) if j == 3 else None)
        pti = 4 + (arot["pt"] % 2)
        arot["pt"] += 1
        pt = tb(pti, 4 * W, p=(0, nk))
        sap, sres = PS((2, 4), (0, W), p=(0, nk))
        act((pt[0].rearrange("p (j w) -> p j w", j=4), pt[1]), (sap, sres), AF.Exp, scale=0.125)
        ma, mr = mask
        g = 4 * npairs
        pv = pt[0].rearrange("p (g q) -> p g q", q=nq)
        tt("dve", (pv, pt[1]), (pv, pt[1]), (ma.unsqueeze(1).broadcast_to([nk, g, nq]), mr), ALU.mult)
        prev = apipe["pending"]
        apipe["pending"] = lambda: pv_part(nk, W, vfn, npairs, udbank, first, last, ucols, pti, post)
        if prev is not None:
            prev()

    def pv_part(nk, W, vfn, npairs, udbank, first, last, ucols, pti, post):
        for j in range(4):
            a, hf = j // 2, j % 2
            pj = tb(pti, W, p=(0, nk), c0=j * W)
            if ucols is not None and npairs > 1:
                pj = (pj[0].rearrange("p (i t) -> p i t", i=npairs), pj[1])
            va, vr = vfn(j)
            if ucols is None:
                uo = PS(udbank, ((0 * 2 + a) * W, W), p=(hf * 64, 64))
                do = PS(udbank, ((1 * 2 + a) * W, W), p=(hf * 64, 64))
                uo = (sq(uo[0], 0), uo[1])
                do = (sq(do[0], 0), do[1])
            else:
                uo = ucols(0, a, hf)
                do = ucols(1, a, hf)
            mm(uo, (va, vr), pj, first, last)
            oa, orr = ONE((0, 64), p=(0, nk))
            mm(do, (oa, orr), pj, first, last)
        if post is not None:
            post()

    def mask_ap(mi, nk, nq):
        ap, res = MK(mi, (0, nq), p=(0, nk))
        return sq(ap, 0), res

    stat_state = {"n": 0, "pending": []}

    def epilogue(oc, ti, c0, n, bank, ln_stats, first_partial=True):
        zap, zres = Z(oc, (c0, n))
        zz = (sq(zap, 0), zres)
        if first_partial:
            stt("dve", zz, xb_cols(oc, c0, n), ALPHA, ps_bank(bank, n), ALU.mult, ALU.add)
        else:
            tt("dve", zz, zz, ps_bank(bank, n), ALU.add)
        if ln_stats:
            k = stat_state["n"] % 2
            stat_state["n"] += 1
            zb = tb(k, n)
            sqb = tb(2 + k, n)
            cp("act", zb, zz)
            act(sqb, zz, AF.Square)
            stat_state["pending"].append((oc, ti, n, zb, sqb))

    def flush_stats():
        for (oc, ti, n, zb, sqb) in stat_state["pending"]:
            oa, orr = ONE((0, 128))
            mm(ps_bank(2 + ti, n), (oa, orr), zb, oc == 0, oc == 15)
            mm(ps_bank(5 + ti, n), (oa, orr), sqb, oc == 0, oc == 15)
        stat_state["pending"] = []

    def ln_finalize(lni, ti, c0, n, final):
        m = tm(0, n)
        rstd = tm(1, n)
        var = tm(4, n)
        ts("dve", m, ps_bank(2 + ti, n), 1.0 / D, ALU.mult)
        tt("dve", var, m, m, ALU.mult)
        stt("dve", var, ps_bank(5 + ti, n), 1.0 / D, var, ALU.mult, ALU.subtract)
        ts("dve", var, var, EPS, ALU.add)
        act(var, var, AF.Sqrt)
        recip(rstd, var)
        for c in range(16):
            t = tm(2 + (c % 2), n)
            zap, zres = Z(c, (c0, n))
            zz = (sq(zap, 0), zres)
            tt("dve", t, zz, m, ALU.subtract)
            tt("dve", t, t, rstd, ALU.mult)
            ga, gr = LNG(lni, c)
            ba, br = LNB(lni, c)
            g1 = (sq(ga, 0), gr)
            b1 = (sq(ba, 0), br)
            act(xb_cols(c, c0, n), t, AF.Identity, bias=b1, scale=g1)
            if final:
                act(zz, t, AF.Identity, bias=b1, scale=g1)

    def dense_ln(nslots_fn, lni, final=False, kc_src=None):
        for pi in range(8):
            slot = next_panel()
            for ocl in range(2):
                oc = pi * 2 + ocl
                for ti, (c0, n) in enumerate(TTS):
                    bank = next_bank(2)
                    for kc in range(16):
                        sa, sr = OT(kc, (c0, n))
                        mm(ps_bank(bank, n), Wl(slot, kc, ocl * 128, 128), (sq(sa, 0), sr), kc == 0, kc == 15)
                    flush_stats()
                    epilogue(oc, ti, c0, n, bank, True)
        flush_stats()
        for ti, (c0, n) in enumerate(TTS):
            ln_finalize(lni, ti, c0, n, final)

    def mlp(l, final):
        lni = 2 * l + 1
        if small == 2:
            return
        for hg in range(4):
            for pi in range(8):
                slot = next_panel()
                for ocl in range(2):
                    hc = pi * 2 + ocl
                    for ti, (c0, n) in enumerate(TTS):
                        bank = next_bank(2)
                        for kc in range(16):
                            mm(ps_bank(bank, n), Wl(slot, kc, ocl * 128, 128), xb_cols(kc, c0, n), kc == 0, kc == 15)
                        r = tm(2 + (rot["i"] % 2), n)
                        act(r, ps_bank(bank, n), AF.Relu)
                        ha, hr = HID(hc, (c0, n))
                        tt("dve", (sq(ha, 0), hr), r, r, ALU.mult)
            for pi in range(8):
                slot = next_panel()
                for ocl in range(2):
                    oc = pi * 2 + ocl
                    for ti, (c0, n) in enumerate(TTS):
                        bank = next_bank(2)
                        for kc in range(16):
                            ha, hr = HID(kc, (c0, n))
                            mm(ps_bank(bank, n), Wl(slot, kc, ocl * 128, 128), (sq(ha, 0), hr), kc == 0, kc == 15)
                        flush_stats()
                        epilogue(oc, ti, c0, n, bank, hg == 3, first_partial=(hg == 0))
        flush_stats()
        for ti, (c0, n) in enumerate(TTS):
            ln_finalize(lni, ti, c0, n, final)

    cid = {}

    def core_vals(e):
        if "pid" not in cid:
            pid = e.partition_id()
            cid["pid"] = pid
            cid["m1"] = (pid + 7) % 8
            cid["m2"] = (pid + 6) % 8
        return cid

    def layer_a(l):
        checkpoint(101 + 1000 * l)
        for i in range(8):
            slot = next_panel()

            def dq(half, c0, n, i=i):
                ap, res = QT(2 * i + half, (c0, n))
                return sq(ap, 0), res
            proj_pair(slot, dq)
            checkpoint(102 + 1000 * l)
        slot = next_panel()

        def dk(half, c0, n):
            ap, res = KT(half, (128 + c0, n))
            return sq(ap, 0), res
        proj_pair(slot, dk)
        checkpoint(103 + 1000 * l)
        slotK = next_panel()
        slotV = next_panel(held=1)
        for ti in range(8):
            c0 = ti * 128
            bank = next_bank(4)
            if ti == 7:
                tok_proj(slotK, c0, 128, bank, 0)
            tok_proj(slotV, c0, 128, bank, 256)
            va, vr = VT(ti + 1)
            cp("act", (sq(va, 0), vr), ps_bank(bank, 256, c0=256))
            if ti == 7:
                st = STG(0)
                sa = sq(st[0], 0)
                rope_tok(ps_bank(bank, 256), 128, 7, (sa[:, 0:256], st[1]))
                cp("act", (sa[:, 256:512], st[1]), ps_bank(bank, 256, c0=256))
                dma("sp", oak_p[l, :, :], sa[:, 0:256], [st[1]], [dram_res("oak_p")])
                dma("sp", oav_p[l, :, :], sa[:, 256:512], [st[1]], [dram_res("oav_p")])
        checkpoint(105 + 1000 * l)
        for s in range(4):
            c0 = TP + 8 * s
            bank = next_bank(4)
            tok_proj(slotK, c0, 8, bank, 0)
            tok_proj(slotV, c0, 8, bank, 256)
            va, vr = VS(s, p=(0, 8))
            cp("act", (sq(va, 0), vr), ps_bank(bank, 256, p=(0, 8), c0=256))
            st = STG(1, p=(0, 8))
            sa = sq(st[0], 0)
            rope_tok(ps_bank(bank, 256, p=(0, 8)), 8, 8, (sa[:, 0:256], st[1]))
            cp("act", (sa[:, 256:512], st[1]), ps_bank(bank, 256, p=(0, 8), c0=256))
            dma("sp", oak_s[l, s, 120:128, :], sa[:, 0:256], [st[1]], [dram_res("oak_s")])
            dma("sp", oav_s[l, s, 120:128, :], sa[:, 256:512], [st[1]], [dram_res("oav_s")])
            dma("sp", oak_s[l, s, 0:120, :], cak[l, s, 8:128, :], [], [dram_res("oak_s")])
            dma("sp", oav_s[l, s, 0:120, :], cav[l, s, 8:128, :], [], [dram_res("oav_s")])
        checkpoint(1 + 10 * l)
        g_in, g_out = gin_a[l], gout_a[l]
        ka, kr = KT((0, 2), (128 + 896, 128))
        dma("sp", g_in[:, 0:256].rearrange("p (h n) -> p h n", h=2), ka, [kr], [dram_res(g_in.name)])
        va, vr = VT(8)
        dma("sp", g_in[:, 256:512], sq(va, 0), [vr], [dram_res(g_in.name)])
        gather(g_in, g_out, 128)
        ka0, kr0 = KT((0, 2), (0, 128))
        va0, vr0 = VT(0)

        def halo_k(e):
            cv = core_vals(e)
            return e.dma_start(out=ka0, in_=g_out[bass.ds(cv["m1"] * 128, 128), 0:256].rearrange("p (h n) -> p h n", h=2))

        def halo_v(e):
            cv = core_vals(e)
            return e.dma_start(out=sq(va0, 0), in_=g_out[bass.ds(cv["m1"] * 128, 128), 256:512])
        E("pool", halo_k, [dram_res(g_out.name)], [kr0], dma=True)
        E("pool", halo_v, [dram_res(g_out.name)], [vr0], dma=True)
        checkpoint(110 + 1000 * l)
        for s in range(4):
            ka, kr = KTOKC(s)
            for half in range(2):
                dma("pool", sq(ka, 0)[:, half * 128:(half + 1) * 128].rearrange("p (h d) -> p h d", h=4),
                    cak[l, s, :, :].rearrange("m (h t d) -> m h t d", h=4, t=2)[:, :, half, :], [], [kr])
            va, vr = VC(s)
            dma("pool", sq(va, 0), cav[l, s, :, :], [], [vr])
            bank = 4 + (s % 2)
            for half in range(2):
                src = sq(ka, 0)[:, half * 128:(half + 1) * 128]
                oap, ores = PSB(bank, (half * 128, 128))
                transpose((sq(oap, 0), ores), (src, kr), IDB((0, 128)))
            oap, ores = PSB(bank, (0, 256))
            da, drr = KCT(s)
            cp("act", (sq(da, 0), drr), (sq(oap, 0).rearrange("p (h n) -> p h n", h=2), ores))
        checkpoint(111 + 1000 * l)
        for b in [1, 2, 3, 4, 5, 6, 7, 0]:
            for i in range(8):
                udb = 6 + (arot["ud"] % 2)
                arot["ud"] += 1

                def qfn(j, half, i=i, b=b):
                    ap, res = QT(2 * i + half, (128 * b, 128), p=(32 * j, 32))
                    return sq(ap, 0), res
                for kb in range(2):
                    def ktfn(j, half, b=b, kb=kb):
                        ap, res = KT(half, (128 * (b + kb), 128), p=(32 * j, 32))
                        return sq(ap, 0), res

                    def vfn(j, b=b, kb=kb):
                        ap, res = VT(b + kb, (64 * j, 64))
                        return sq(ap, 0), res
                    mi = (3 if b == 0 else 1) if kb == 0 else 0

                    def post(i=i, b=b, udb=udb):
                        W = 128
                        ud = tm(5, 4 * W)
                        cp("act", ud, ps_bank(udb, 4 * W))
                        rd = RD((0, 2 * W))
                        for a in range(2):
                            sa_, sr_ = SK(l, 2 * i + a)
                            ra_, rr_ = RD((a * W, W))
                            ts("dve", (ra_, rr_), (ud[0][:, (2 + a) * W:(3 + a) * W], ud[1]), (sq(sa_, 0), sr_), ALU.add)
                        recip(rd, rd)
                        for a in range(2):
                            oa, orr = OT(2 * i + a, (128 * b, 128))
                            ra_, rr_ = RD((a * W, W))
                            tt("dve", (sq(oa, 0), orr), (ud[0][:, a * W:(a + 1) * W], ud[1]), (ra_, rr_), ALU.mult)
                    attn_kb(128, 128, qfn, ktfn, vfn, mask_ap(mi, 128, 128), 128, 1, udb, kb == 0, kb == 1,
                            post=post if kb == 1 else None)
        attn_flush()
        checkpoint(113 + 1000 * l)
        for s in range(4):
            udb = 6 + (arot["ud"] % 2)
            arot["ud"] += 1
            W = 64

            def qfn(j, half, s=s):
                ap, res = QT((half, 8, 2), (TP + 8 * s, 8), p=(32 * j, 32))
                return ap, res
            for kb in range(2):
                if kb == 0:
                    def ktfn(j, half, s=s):
                        ap, res = KCT(s, half, p=(32 * j, 32))
                        return sq(ap, 0, 1), res

                    def vfn(j, s=s):
                        ap, res = VC(s, (64 * j, 64))
                        return sq(ap, 0), res
                    nk, mi = 128, 1
                else:
                    def ktfn(j, half, s=s):
                        ap, res = KT(half, (128 + TP + 8 * s, 8), p=(32 * j, 32))
                        return sq(ap, 0), res

                    def vfn(j, s=s):
                        ap, res = VS(s, (64 * j, 64), p=(0, 8))
                        return sq(ap, 0), res
                    nk, mi = 8, 0
                def post(s=s, udb=udb, W=W):
                    ud = tm(5, 4 * W)
                    cp("act", ud, ps_bank(udb, 4 * W))
                    rd = RD((0, 2 * W))
                    for a in range(2):
                        for i in range(8):
                            sa_, sr_ = SK(l, 2 * i + a)
                            ra_, rr_ = RD((a * W + 8 * i, 8))
                            ts("dve", (ra_, rr_), (ud[0][:, (2 + a) * W + 8 * i:(2 + a) * W + 8 * i + 8], ud[1]),
                               (sq(sa_, 0), sr_), ALU.add)
                    recip(rd, rd)
                    for a in range(2):
                        oa, orr = OT((a, 8, 2), (TP + 8 * s, 8))
                        ra_, rr_ = RD((a * W, W))
                        tt("dve", (oa, orr), (ud[0][:, a * W:(a + 1) * W].rearrange("p (i t) -> p i t", i=8), ud[1]),
                           (ra_.rearrange("p (i t) -> p i t", i=8), rr_), ALU.mult)
                attn_kb(nk, W, qfn, ktfn, vfn, mask_ap(mi, nk, 8), 8, 8, udb, kb == 0, kb == 1,
                        post=post if kb == 1 else None)
        attn_flush()
        checkpoint(2 + 10 * l)
        dense_ln(None, 2 * l)
        checkpoint(3 + 10 * l)
        mlp(l, False)
        checkpoint(4 + 10 * l)

    def kv_b():
        slot = next_panel()

        def dk(half, c0, n):
            ap, res = KTB(half, (2048 + c0, n))
            return sq(ap, 0), res
        proj_pair(slot, dk)
        slotK = next_panel()
        slotV = next_panel(held=1)
        for ti in range(8):
            c0 = ti * 128
            bank = next_bank(4)
            tok_proj(slotK, c0, 128, bank, 0)
            tok_proj(slotV, c0, 128, bank, 256)
            st = STG(ti % 2)
            sa = sq(st[0], 0)
            rope_tok(ps_bank(bank, 256), 128, ti, (sa[:, 0:256], st[1]))
            cp("act", (sa[:, 256:512], st[1]), ps_bank(bank, 256, c0=256))
            dma("sp", obk_p[c0:c0 + 128, :], sa[:, 0:256], [st[1]], [dram_res("obk_p")])
            dma("sp", obv_p[c0:c0 + 128, :], sa[:, 256:512], [st[1]], [dram_res("obv_p")])
            dma("pool", gin_bv[c0:c0 + 128, :], sa[:, 256:512], [st[1]], [dram_res("gin_bv")])
        for s in range(4):
            c0 = TP + 8 * s
            bank = next_bank(4)
            tok_proj(slotK, c0, 8, bank, 0)
            tok_proj(slotV, c0, 8, bank, 256)
            st = STG(s % 2, p=(0, 8))
            sa = sq(st[0], 0)
            rope_tok(ps_bank(bank, 256, p=(0, 8)), 8, 8, (sa[:, 0:256], st[1]))
            cp("act", (sa[:, 256:512], st[1]), ps_bank(bank, 256, p=(0, 8), c0=256))
            dma("sp", obk_s[s, 2040:2048, :], sa[:, 0:256], [st[1]], [dram_res("obk_s")])
            dma("sp", obv_s[s, 2040:2048, :], sa[:, 256:512], [st[1]], [dram_res("obv_s")])
            dma("pool", vbs_scr[:, 256 * s:256 * s + 256], sa[:, 256:512], [st[1]], [dram_res("vbs_scr")])
            for q4 in range(4):
                r0 = 8 + 510 * q4
                dma("sp", obk_s[s, r0 - 8:r0 + 502, :], cbk[s, r0:r0 + 510, :], [], [dram_res("obk_s")])
                dma("sp", obv_s[s, r0 - 8:r0 + 502, :], cbv[s, r0:r0 + 510, :], [], [dram_res("obv_s")])
        ka, kr = KTB((0, 2), (2048, TP))
        dma("sp", gin_bk[:, :].rearrange("p (h n) -> p h n", h=2), ka, [kr], [dram_res("gin_bk")])
        ka, kr = KTB((0, 2), (3072, TS))
        dma("sp", kbs_scr[:, :].rearrange("p (h n) -> p h n", h=2), ka, [kr], [dram_res("kbs_scr")])
        gather(gin_bk, gout_bk, 128)
        gather(gin_bv, gout_bv, TP)

    def load_kv_b():
        def dyn(out_ap, res, gbuf, which, rows_per, r0, nrows, rearr=None, **kw):
            def f(e):
                cv = core_vals(e)
                src = gbuf[bass.ds(cv[which] * rows_per + r0, nrows), :]
                if rearr is not None:
                    src = src.rearrange(rearr, **kw)
                return e.dma_start(out=out_ap, in_=src)
            E("pool", f, [dram_res(gbuf.name)], [res], dma=True)
        for ci, which in enumerate(("m2", "m1", "pid")):
            ka, kr = KTB((0, 2), (1024 * ci, 1024))
            dyn(ka, kr, gout_bk, which, 128, 0, 128, "p (h n) -> p h n", h=2)
        ka, kr = KTB((0, 2), (3072, TS))
        dma("pool", ka, kbs_scr[:, :].rearrange("p (h n) -> p h n", h=2), [dram_res("kbs_scr")], [kr])
        va, vr = VB1(0)
        dyn(sq(va, 0), vr, gout_bv, "m1", 1024, 896, 128)
        va, vr = VB1((1, 8))
        dyn(va, vr, gout_bv, "pid", 1024, 0, 1024, "(b p) f -> p b f", p=128)
        for blk, (which, r0) in enumerate((("m1", 512), ("pid", 0), ("pid", 512))):
            va, vr = VB4((0, 4), blk)
            dyn(sq(va, 1), vr, gout_bv, which, 1024, r0, 512, "(m r) f -> m r f", r=4)
        va, vr = VB16((0, 16), 0, p=(0, 64))
        dyn(sq(va, 1), vr, gout_bv, "m2", 1024, 0, 1024, "(m r) f -> m r f", r=16)
        va, vr = VB16((0, 16), 0, p=(64, 64))
        dyn(sq(va, 1), vr, gout_bv, "m1", 1024, 0, 1024, "(m r) f -> m r f", r=16)
        va, vr = VB16((0, 16), 1, p=(0, 64))
        dyn(sq(va, 1), vr, gout_bv, "pid", 1024, 0, 1024, "(m r) f -> m r f", r=16)

    def layer_b(l):
        if l == 2:
            kv_b()
        load_kv_b()
        blocks = {0: [], 1: [], 2: []}
        for b in range(8):
            kbs = []
            for kb in range(2):
                def vfn(j, b=b, kb=kb):
                    ap, res = VB1(b + kb, (64 * j, 64))
                    return sq(ap, 0), res
                mi = (4 if b == 0 else 2) if kb == 0 else 0
                kbs.append((1920 + 128 * (b + kb), 1, 128, vfn, mi))
            blocks[0].append((128, 128 * b, 1, kbs))
        for r in range(4):
            for b in range(2):
                kbs = []
                for kb in range(2):
                    def vfn(j, r=r, b=b, kb=kb):
                        ap, res = VB4(r, b + kb, (64 * j, 64))
                        return sq(ap, 0, 1), res
                    mi = (4 if b == 0 else 2) if kb == 0 else 0
                    kbs.append((1536 + 512 * (b + kb) + r, 4, 128, vfn, mi))
                blocks[1].append((128, 512 * b + r, 4, kbs))
        for r in range(16):
            kbs = []
            for kb in range(2):
                nk = 128 if kb == 0 else 64

                def vfn(j, r=r, kb=kb, nk=nk):
                    ap, res = VB16(r, kb, (64 * j, 64), p=(0, nk))
                    return sq(ap, 0, 1), res
                kbs.append((2048 * kb + r, 16, nk, vfn, 5 if kb == 0 else 0))
            blocks[2].append((64, r, 16, kbs))

        jb = l - 2
        for i in range(8):
            for g in range(3):
                slot = next_panel()
                qb = (i * 3 + g) % 2

                def dq(half, c0, n, qb=qb, g=g, i=i):
                    if c0 >= TP:
                        ap, res = QS(g, i, half)
                        return sq(ap, 0, 1, 2), res
                    ap, res = QBUF(qb, half, (c0, n))
                    return sq(ap, 0, 1), res
                proj_pair(slot, dq, fixed=True)
                for (nq, q0, qst, kbs) in blocks[g]:
                    udb = 6 + (arot["ud"] % 2)
                    arot["ud"] += 1

                    def qfn(j, half, qb=qb, q0=q0, qst=qst, nq=nq):
                        ap, res = QBUF(qb, half, (q0, nq, qst), p=(32 * j, 32))
                        return sq(ap, 0, 1), res
                    for kbi, (k0, kst, nk, vfn, mi) in enumerate(kbs):
                        def ktfn(j, half, k0=k0, kst=kst, nk=nk):
                            ap, res = KTB(half, (k0, nk, kst), p=(32 * j, 32))
                            return sq(ap, 0), res
                        def post(udb=udb, nq=nq, q0=q0, qst=qst, g=g):
                            ud = tm(5, 4 * nq)
                            cp("act", ud, ps_bank(udb, 4 * nq))
                            src = ud[0].rearrange("p (u a q) -> p u a q", u=2, a=2)
                            aa, ar = ACC((0, 2), (0, 2), (q0, nq, qst))
                            if g == 0:
                                cp("dve", (aa, ar), (src, ud[1]))
                            else:
                                tt("dve", (aa, ar), (aa, ar), (src, ud[1]), ALU.add)
                        attn_kb(nk, nq, qfn, ktfn, vfn, mask_ap(mi, nk, nq), nq, 1, udb, kbi == 0, kbi == 1,
                                post=post if kbi == 1 else None)
            attn_flush()
            da, drr = ACC(1)
            recip((sq(da, 0), drr), (sq(da, 0), drr))
            ua, ur = ACC(0)
            oa, orr = OT((2 * i, 2), (0, TP))
            tt("dve", (oa, orr), (sq(ua, 0), ur), (sq(da, 0), drr), ALU.mult)
        va, vr = VSB((0, 4), p=(0, 8))
        dma("pool", va, vbs_scr[:, :].rearrange("p (s f) -> p s f", s=4), [dram_res("vbs_scr")], [vr])
        for s in range(4):
            a0, r0_ = VCB(0)
            dma("pool", sq(a0, 0), cbv[s, 1920:2048, :], [], [r0_])
            a1, r1_ = VCB((1, 4))
            dma("pool", a1, cbv[s, 1536:2048, :].rearrange("(m r) f -> m r f", r=4), [], [r1_])
            a2, r2_ = VCB((5, 8))
            dma("pool", a2, cbv[s, :, :].rearrange("(m r) f -> m r f", r=16)[:, 0:8, :], [], [r2_])
            rowsel = [(1920, 1)] + [(1536 + r, 4) for r in range(4)] + [(t, 16) for t in range(8)]
            for blk, (rs, rstep) in enumerate(rowsel):
                a0, r0_ = KTOKB(blk)
                for half in range(2):
                    dma("pool", sq(a0, 0)[:, half * 128:(half + 1) * 128].rearrange("p (h d) -> p h d", h=4),
                        cbk[s, rs:rs + 127 * rstep + 1:rstep, :].rearrange("m (h t d) -> m h t d", h=4, t=2)[:, :, half, :],
                        [], [r0_])
            for blk in range(13):
                bank = 4 + (blk % 2)
                ka, kr = KTOKB(blk)
                for half in range(2):
                    src = sq(ka, 0)[:, half * 128:(half + 1) * 128]
                    oap, ores = PSB(bank, (half * 128, 128))
                    transpose((sq(oap, 0), ores), (src, kr), IDB((0, 128)))
                oap, ores = PSB(bank, (0, 256))
                da, drr = KCTB(blk)
                cp("act" if blk % 2 == 0 else "dve", (sq(da, 0), drr),
                   (sq(oap, 0).rearrange("p (h n) -> p h n", h=2), ores))
            udb = 6 + (arot["ud"] % 2)
            arot["ud"] += 1
            sblocks = [(0, 0, 1, 8, 0)] + [(1, r, 4, 2, 1 + r) for r in range(4)] + [(2, t, 1, 1, 5 + t) for t in range(8)]
            cur = [(0, 0), (1, 6), (2, 7)]
            nb_total = len(sblocks) + len(cur)
            cnt = 0

            def ucols_fn(t0, tst, nq):
                def f(ud, a, hf):
                    ap, res = PS(udb, ((ud * 2 + a) * 64, 64), p=(hf * 64, 64))
                    v = sq(ap, 0).rearrange("p (i t) -> p i t", i=8)[:, :, t0:t0 + (nq - 1) * tst + 1:tst]
                    return v, res
                return f
            for (g, t0, tst, nq, blk) in sblocks:
                def qfn(j, half, g=g, t0=t0, tst=tst, nq=nq, s=s):
                    ap, res = QS(g, (0, 8), half, (8 * s + t0, nq, tst), p=(32 * j, 32))
                    return sq(ap, 0, 2), res

                def ktfn(j, half, blk=blk):
                    ap, res = KCTB(blk, half, p=(32 * j, 32))
                    return sq(ap, 0, 1), res

                def vfn(j, blk=blk):
                    ap, res = VCB(blk, (64 * j, 64))
                    return sq(ap, 0), res
                attn_kb(128, 8 * nq, qfn, ktfn, vfn, mask_ap(2, 128, nq), nq, 8, udb, cnt == 0, cnt == nb_total - 1,
                        ucols=ucols_fn(t0, tst, nq))
                cnt += 1
            for (g, mi) in cur:
                def qfn(j, half, g=g, s=s):
                    ap, res = QS(g, (0, 8), half, (8 * s, 8), p=(32 * j, 32))
                    return sq(ap, 0, 2), res

                def ktfn(j, half, s=s):
                    ap, res = KTB(half, (3072 + 8 * s, 8), p=(32 * j, 32))
                    return sq(ap, 0), res

                def vfn(j, s=s):
                    ap, res = VSB(s, (64 * j, 64), p=(0, 8))
                    return sq(ap, 0), res
                attn_kb(8, 64, qfn, ktfn, vfn, mask_ap(mi, 8, 8), 8, 8, udb, False, cnt == nb_total - 1,
                        ucols=ucols_fn(0, 1, 8))
                cnt += 1
            attn_flush()
            W = 64
            ud = tm(5, 4 * W)
            cp("act", ud, ps_bank(udb, 4 * W))
            rd = RD((0, 2 * W))
            recip(rd, (ud[0][:, 2 * W:4 * W], ud[1]))
            for a in range(2):
                oa, orr = OT((a, 8, 2), (TP + 8 * s, 8))
                ra_, rr_ = RD((a * W, W))
                tt("dve", (oa, orr), (ud[0][:, a * W:(a + 1) * W].rearrange("p (i t) -> p i t", i=8), ud[1]),
                   (ra_.rearrange("p (i t) -> p i t", i=8), rr_), ALU.mult)
        checkpoint(2 + 10 * l)
        dense_ln(None, 2 * l)
        checkpoint(3 + 10 * l)
        mlp(l, l == 3)
        checkpoint(4 + 10 * l)

    def main_body():
        checkpoint(100)
        layer_a(0)
        layer_a(1)
        layer_b(2)
        layer_b(3)
        output_phase()

    def output_phase():
        for ti, (r0, nt) in enumerate(TOKT):
            sbi = ti % 2
            for qd in range(4):
                bank = next_bank(4)
                for k in range(4):
                    c = qd * 4 + k
                    zap, zres = Z(c, (r0, nt))
                    oap, ores = PS(bank, (k * 128, 128), p=(0, nt))
                    transpose((sq(oap, 0), ores), (sq(zap, 0), zres), IDF((0, 128)))
                oap, ores = PS(bank, (0, 512), p=(0, nt))
                dap, dres = OSTG(sbi, (qd * 512, 512), p=(0, nt))
                cp("act" if qd % 2 == 0 else "dve", (sq(dap, 0), dres), (sq(oap, 0), ores))
            dap, dres = OSTG(sbi, p=(0, nt))
            dma("sp", y[r0:r0 + nt, :], sq(dap, 0), [dres], [dram_res("y")])

    try:
        main_body()
    except _Stop:
        pass
    tr.finalize()

    sems = {}
    engs = ["pe", "act", "dve", "pool", "sp"]
    nms = {}
    for op in tr.ops:
        if op.ms:
            nms[op.eng] = max(nms.get(op.eng, 0), op.ms)
    for e in engs:
        for k in range((nms.get(e, 0) + SEMCH - 1) // SEMCH + 1):
            sems[("eng", e, k)] = es.enter_context(nc.semaphore("s_%s_%d" % (e, k)))
    for q in ("pool", "sp"):
        for s in range(NDMASEM):
            sems[("dma", (q, s))] = es.enter_context(nc.semaphore("d_%s_%d" % (q, s)))
    for s in range(tr.ncoll):
        sems[("coll", s)] = es.enter_context(nc.semaphore("c_%d" % s))
    block = es.enter_context(nc.Block())
    per_eng = {e: [op for op in tr.ops if op.eng == e] for e in engs}

    def run_queue(eobj, name):
        for op in per_eng[name]:
            for (key, v) in op.waits:
                eobj.wait_ge(sems[key], v)
            inst = op.fn(eobj)
            if op.coll:
                inst.then_inc(sems[op.sem])
            elif op.dma:
                inst.then_inc(sems[op.sem], 16)
            elif op.ms is not None:
                inst.then_inc(sems[("eng", name, (op.ms - 1) // SEMCH)], 1)
        if name == "sp":
            for s, c in tr.dma_count.items():
                eobj.wait_ge(sems[("dma", s)], 16 * c)

    @block.tensor
    def _(e):
        run_queue(e, "pe")

    @block.scalar
    def _(e):
        run_queue(e, "act")

    @block.vector
    def _(e):
        run_queue(e, "dve")

    @block.gpsimd
    def _(e):
        run_queue(e, "pool")

    @block.sync
    def _(e):
        run_queue(e, "sp")

    es.close()
    return nc


_NC_CACHE = {}


def prep_inputs(x_prompt, x_sample, cache_a_k, cache_a_v, cache_b_k, cache_b_v, ln_g, ln_b,
                w_qkv_a, sinks_a, w_o_a, w_kv_b, w_q_b, w_o_b, w_up, w_down):
    f32 = np.float32
    qp = qperm_cols()
    kp = kperm_cols()
    orow = operm_rows()
    x_prompt = np.asarray(x_prompt, f32)
    x_sample = np.asarray(x_sample, f32)
    w_qkv_a = np.asarray(w_qkv_a, f32)
    w_q_b = np.asarray(w_q_b, f32)
    w_kv_b = np.asarray(w_kv_b, f32)
    wqa = np.ascontiguousarray(np.concatenate([w_qkv_a[:, :, :2048][:, :, qp], w_qkv_a[:, :, 2048:2304][:, :, kp]], axis=2))
    wkva = np.ascontiguousarray(w_qkv_a[:, :, 2048:2560])
    woa = np.ascontiguousarray(np.asarray(w_o_a, f32)[:, orow, :])
    wkvb = np.ascontiguousarray(np.concatenate([w_kv_b[:, :256][:, kp], w_kv_b], axis=1))
    wqb = np.ascontiguousarray(np.concatenate([w_q_b[:, :, g * 2048:(g + 1) * 2048][:, :, qp] for g in range(3)], axis=2))
    wob = np.ascontiguousarray(np.asarray(w_o_b, f32)[:, orow, :])
    wup = np.ascontiguousarray(np.asarray(w_up, f32))
    wdn = np.ascontiguousarray(np.asarray(w_down, f32))
    lng = np.ascontiguousarray(np.asarray(ln_g, f32).reshape(8, 16, 128).transpose(2, 0, 1).reshape(128, 128))
    lnb = np.ascontiguousarray(np.asarray(ln_b, f32).reshape(8, 16, 128).transpose(2, 0, 1).reshape(128, 128))
    sk = np.zeros((128, 2, 16), f32)
    sa = np.asarray(sinks_a, f32)
    for i in range(8):
        for a in range(2):
            for hf in range(2):
                sk[hf * 64:(hf + 1) * 64, :, 2 * i + a] = sa[:, 8 * (2 * a + hf) + i][None, :]
    sk = sk.reshape(128, 32)
    identb = np.eye(128, dtype=f32).astype(ml_dtypes.bfloat16)
    identf = np.eye(128, dtype=f32)
    kk = np.arange(128)[:, None]
    qq = np.arange(128)[None, :]
    mcur = (kk <= qq).astype(f32)
    mpa = (kk > qq).astype(f32)
    mpb = (kk >= qq).astype(f32)
    m4 = np.zeros((128, 128), f32)
    m16 = np.zeros((128, 128), f32)
    for k_ in range(8):
        for t_ in range(8):
            if k_ <= t_ and (t_ - k_) % 4 == 0:
                m4[k_, t_] = 1
            if k_ == t_:
                m16[k_, t_] = 1
    cak = np.asarray(cache_a_k, f32).reshape(2, 32, 128, 256)
    cav = np.asarray(cache_a_v, f32).reshape(2, 32, 128, 256)
    cbk = np.asarray(cache_b_k, f32).reshape(32, 2048, 256)
    cbv = np.asarray(cache_b_v, f32).reshape(32, 2048, 256)

    in_maps = []
    for c in range(NCORE):
        b, ch = c // 4, c % 4
        c0 = ch * TP
        xin = np.concatenate([x_prompt[b, c0:c0 + TP], x_sample[4 * c:4 * c + 4].reshape(TS, D)], axis=0)
        pos = np.concatenate([np.arange(c0, c0 + TP), PAST + np.tile(np.arange(8), 4)]).astype(np.int64)
        cs, sn = rope_tables(pos)
        cosf = np.tile(cs.T, (4, 1))
        sinf = np.tile(sn.T, (4, 1))
        cost = np.zeros((128, 9, 32), f32)
        sint = np.zeros((128, 9, 32), f32)
        for ti in range(8):
            cost[:, ti] = cs[ti * 128:(ti + 1) * 128]
            sint[:, ti] = sn[ti * 128:(ti + 1) * 128]
        cost[0:8, 8] = cs[TP:TP + 8]
        sint[0:8, 8] = sn[TP:TP + 8]
        v1 = 1.0 if ch >= 1 else 0.0
        v2 = 1.0 if ch >= 2 else 0.0
        m5 = mpb * np.where(kk < 64, v2, v1)
        mk = np.stack([mcur, mpa, mpb, mpa * v1, mpb * v1, m5, m4, m16], axis=1).reshape(128, 8 * 128)
        in_maps.append({
            "x_in": np.ascontiguousarray(xin),
            "cak": np.ascontiguousarray(cak[:, 4 * c:4 * c + 4]),
            "cav": np.ascontiguousarray(cav[:, 4 * c:4 * c + 4]),
            "cbk": np.ascontiguousarray(cbk[4 * c:4 * c + 4]),
            "cbv": np.ascontiguousarray(cbv[4 * c:4 * c + 4]),
            "lng": lng, "lnb": lnb, "sinks": sk,
            "wqa": wqa, "wkva": wkva, "woa": woa, "wkvb": wkvb, "wqb": wqb, "wob": wob, "wup": wup, "wdn": wdn,
            "cosf": np.ascontiguousarray(cosf), "sinf": np.ascontiguousarray(sinf),
            "cost": cost.reshape(128, 288), "sint": sint.reshape(128, 288),
            "masks": mk.astype(ml_dtypes.bfloat16), "identb": identb, "identf": identf,
        })
    return in_maps


def kernel(**inputs):
    in_maps = prep_inputs(**inputs)
    if "nc" not in _NC_CACHE:
        _NC_CACHE["nc"] = build_program()
    nc = _NC_CACHE["nc"]
    res = run_bass_kernel_spmd(nc, in_maps, core_ids=list(range(NCORE)))
    return assemble(res.results)


def assemble(R):
    y_prompt = np.stack([np.concatenate([R[4 * b + k]["y"][:TP] for k in range(4)], axis=0) for b in range(2)], axis=0)
    y_sample = np.concatenate([R[c]["y"][TP:].reshape(4, 8, D) for c in range(NCORE)], axis=0)
    nak_p = np.stack([np.stack([R[4 * b + 3]["oak_p"][l] for b in range(2)], 0) for l in range(2)], 0).reshape(2, 2, 128, 4, 64)
    nav_p = np.stack([np.stack([R[4 * b + 3]["oav_p"][l] for b in range(2)], 0) for l in range(2)], 0).reshape(2, 2, 128, 4, 64)
    nak_s = np.concatenate([R[c]["oak_s"] for c in range(NCORE)], axis=1).reshape(2, 32, 128, 4, 64)
    nav_s = np.concatenate([R[c]["oav_s"] for c in range(NCORE)], axis=1).reshape(2, 32, 128, 4, 64)
    nbk_p = np.stack([np.concatenate([R[4 * b + 2]["obk_p"], R[4 * b + 3]["obk_p"]], 0) for b in range(2)], 0).reshape(2, 2048, 4, 64)
    nbv_p = np.stack([np.concatenate([R[4 * b + 2]["obv_p"], R[4 * b + 3]["obv_p"]], 0) for b in range(2)], 0).reshape(2, 2048, 4, 64)
    nbk_s = np.concatenate([R[c]["obk_s"] for c in range(NCORE)], axis=0).reshape(32, 2048, 4, 64)
    nbv_s = np.concatenate([R[c]["obv_s"] for c in range(NCORE)], axis=0).reshape(32, 2048, 4, 64)
    outs = (y_prompt, y_sample, nak_p, nav_p, nak_s, nav_s, nbk_p, nbv_p, nbk_s, nbv_s)
    return tuple(np.ascontiguousarray(o, dtype=np.float32) for o in outs)
```

```python
import numpy as np
import ml_dtypes
from contextlib import ExitStack
import concourse.bass as bass
import concourse.mybir as mybir
from concourse.bass_utils import run_bass_kernel_spmd

F32 = mybir.dt.float32
BF16 = mybir.dt.bfloat16
ALU = mybir.AluOpType
AF = mybir.ActivationFunctionType

D = 2048
NCH = 16
TP = 1024
TS = 32
T = TP + TS
DFF = 8192
HD = 64
NCORE = 8
PAST = 16384
ALPHA = 8.0 ** 0.25
EPS = 1e-5
TTS = [(0, 352), (352, 352), (704, 352)]
NSLOT = 4
PW = 256
NDMASEM = 24
SEMCH = 1000


class Op:
    __slots__ = ("eng", "fn", "dma", "deps", "idx", "ms", "sem", "val", "waits", "coll")

    def __init__(self, eng, fn, dma):
        self.eng = eng
        self.fn = fn
        self.dma = dma
        self.deps = set()
        self.ms = None
        self.sem = None
        self.val = None
        self.waits = []
        self.coll = False


class Tracker:
    def __init__(self):
        self.ops = []
        self.bufs = {}
        self.dma_last = {}
        self.dma_count = {}
        self.ndma = 0
        self.ndma_q = {}
        self.ncoll = 0

    def add(self, eng, fn, reads=(), writes=(), dma=False, coll=False):
        op = Op(eng, fn, dma or coll)
        op.coll = coll
        op.idx = len(self.ops)
        for (b, lo, hi) in reads:
            ents = self.bufs.setdefault(b, [])
            for e in ents:
                if e[0] < hi and lo < e[1]:
                    if e[2] is not None:
                        op.deps.add(e[2])
                    if not op.dma:
                        e[3][:] = [r for r in e[3] if r.dma or r.eng != op.eng]
                    e[3].append(op)
        for (b, lo, hi) in writes:
            ents = self.bufs.setdefault(b, [])
            keep = []
            for e in ents:
                if e[0] < hi and lo < e[1]:
                    if e[2] is not None:
                        op.deps.add(e[2])
                    for r in e[3]:
                        op.deps.add(r)
                    if lo <= e[0] and e[1] <= hi:
                        continue
                keep.append(e)
            keep.append([lo, hi, op, []])
            self.bufs[b] = keep
        op.deps.discard(op)
        if coll:
            op.sem = ("coll", self.ncoll)
            op.val = 1
            self.ncoll += 1
        elif dma:
            k = self.ndma_q.get(eng, 0)
            self.ndma_q[eng] = k + 1
            s = (eng, k % NDMASEM)
            self.ndma += 1
            prev = self.dma_last.get(s)
            if prev is not None:
                op.deps.add(prev)
            self.dma_last[s] = op
            self.dma_count[s] = self.dma_count.get(s, 0) + 1
            op.sem = ("dma", s)
            op.val = 16 * self.dma_count[s]
        self.ops.append(op)
        return op

    def finalize(self):
        def needs(d, op):
            return (not d.dma) and (d.eng != op.eng or op.eng != "pe")
        for op in self.ops:
            for d in op.deps:
                if needs(d, op):
                    d.ms = 0
        cnt = {}
        for op in self.ops:
            if op.ms is not None:
                cnt[op.eng] = cnt.get(op.eng, 0) + 1
                op.ms = cnt[op.eng]
        waited = {}
        for op in self.ops:
            need = {}
            for d in op.deps:
                if d.dma:
                    key = d.sem
                    v = d.val
                elif needs(d, op):
                    key = ("eng", d.eng)
                    v = d.ms
                else:
                    continue
                if v > need.get(key, 0):
                    need[key] = v
            w = waited.setdefault(op.eng, {})
            for key, v in need.items():
                if v > w.get(key, 0):
                    w[key] = v
                    if key[0] == "eng":
                        op.waits.append((("eng", key[1], (v - 1) // SEMCH), (v - 1) % SEMCH + 1))
                    else:
                        op.waits.append((key, v))


class Region:
    def __init__(self, buf, handle, handle_dt_size, byte_off, dt, shape):
        self.buf = buf
        self.h = handle
        self.hs = handle_dt_size
        self.off = byte_off
        self.dt = dt
        self.es = 2 if dt == BF16 else 4
        self.shape = tuple(shape)
        n = 1
        for s in shape:
            n *= s
        self.n = n
        base = handle[:, byte_off // handle_dt_size:(byte_off + n * self.es) // handle_dt_size]
        if (dt == BF16) != (handle_dt_size == 2):
            base = base.bitcast(dt)
        if len(shape) == 1:
            self.base = base
        else:
            names = "abcdefg"[:len(shape)]
            kw = {names[i]: shape[i] for i in range(len(shape))}
            self.base = base.rearrange("p (%s) -> p %s" % (" ".join(names), " ".join(names)), **kw)

    def __call__(self, *idx, p=None):
        sl = []
        lo = 0
        hi = 0
        stride = self.n
        for d, ix in enumerate(idx):
            stride //= self.shape[d]
            if isinstance(ix, int):
                sl.append(slice(ix, ix + 1))
                lo += ix * stride
                hi += ix * stride
            else:
                st, cnt = ix[0], ix[1]
                step = ix[2] if len(ix) > 2 else 1
                sl.append(slice(st, st + (cnt - 1) * step + 1, step))
                lo += st * stride
                hi += (st + (cnt - 1) * step) * stride
        for d in range(len(idx), len(self.shape)):
            stride //= self.shape[d]
            sl.append(slice(0, self.shape[d]))
            hi += (self.shape[d] - 1) * stride
        ps = slice(0, 128) if p is None else slice(p[0], p[0] + p[1])
        ap = self.base[(ps,) + tuple(sl)]
        blo, bhi = self.off + lo * self.es, self.off + (hi + 1) * self.es
        if self.buf == "PS":
            blo = (blo // 2048) * 2048
            bhi = ((bhi + 2047) // 2048) * 2048
        return ap, (self.buf, blo, bhi)


def sq(ap, *axes):
    for a in sorted(axes, reverse=True):
        ap = ap.squeeze(a + 1)
    return ap


def qperm_cols(nheads_groups=1):
    cols = []
    for i in range(8):
        for half in range(2):
            for j in range(4):
                h = 8 * j + i
                cols.extend(range(h * 64 + half * 32, h * 64 + half * 32 + 32))
    return np.array(cols)


def kperm_cols():
    cols = []
    for half in range(2):
        for j in range(4):
            cols.extend(range(j * 64 + half * 32, j * 64 + half * 32 + 32))
    return np.array(cols)


def operm_rows():
    rows = []
    for i in range(8):
        for a in range(2):
            for p in range(128):
                h = 8 * (2 * a + p // 64) + i
                rows.append(h * 64 + p % 64)
    return np.array(rows)


def rope_tables(pos):
    half = HD // 2
    inv = (np.float32(10000.0) ** (-np.arange(half, dtype=np.float32) / np.float32(half))).astype(np.float32)
    ang = (pos.astype(np.float32)[:, None] * inv[None, :]).astype(np.float32)
    return np.cos(ang).astype(np.float32), np.sin(ang).astype(np.float32)


class _Stop(Exception):
    pass


def build_program(stage=None, fake_gather=False, small=False):
    nc = bass.Bass(target_bir_lowering=False)

    def checkpoint(n):
        if stage == n:
            raise _Stop()

    tr = Tracker()
    dr = {}

    def din(name, shape, dt=F32):
        dr[name] = nc.dram_tensor(name, list(shape), dt, kind="ExternalInput")
        return dr[name]

    def dout(name, shape, dt=F32):
        dr[name] = nc.dram_tensor(name, list(shape), dt, kind="ExternalOutput")
        return dr[name]

    def dscr(name, shape, dt):
        dr[name] = nc.dram_tensor(name, list(shape), dt)
        return dr[name]

    x_in = din("x_in", [T, D])
    cak = din("cak", [2, 4, 128, 256])
    cav = din("cav", [2, 4, 128, 256])
    cbk = din("cbk", [4, 2048, 256])
    cbv = din("cbv", [4, 2048, 256])
    lng = din("lng", [128, 8 * 16])
    lnb = din("lnb", [128, 8 * 16])
    sinks = din("sinks", [128, 32])
    wqa = din("wqa", [2, D, 2304])
    wkva = din("wkva", [2, D, 512])
    woa = din("woa", [2, D, D])
    if small == 2:
        wkvb = din("wkvb", [D, 768])
        wqb = din("wqb", [2, D, 6144])
        wob = din("wob", [2, D, D])
        wup = din("wup", [4, 128, 256])
        wdn = din("wdn", [4, 128, 256])
    elif small:
        wkvb = din("wkvb", [128, 768])
        wqb = din("wqb", [2, 128, 256])
        wob = din("wob", [2, 128, 256])
        wup = din("wup", [4, 128, 256])
        wdn = din("wdn", [4, 128, 256])
    else:
        wkvb = din("wkvb", [D, 768])
        wqb = din("wqb", [2, D, 6144])
        wob = din("wob", [2, D, D])
        wup = din("wup", [4, D, DFF])
        wdn = din("wdn", [4, DFF, D])
    cosf = din("cosf", [128, T])
    sinf = din("sinf", [128, T])
    cost = din("cost", [128, 9 * 32])
    sint = din("sint", [128, 9 * 32])
    masks = din("masks", [128, 8 * 128], BF16)
    identb = din("identb", [128, 128], BF16)
    identf = din("identf", [128, 128])

    y = dout("y", [T, D])
    oak_p = dout("oak_p", [2, 128, 256])
    oav_p = dout("oav_p", [2, 128, 256])
    oak_s = dout("oak_s", [2, 4, 128, 256])
    oav_s = dout("oav_s", [2, 4, 128, 256])
    obk_p = dout("obk_p", [TP, 256])
    obv_p = dout("obv_p", [TP, 256])
    obk_s = dout("obk_s", [4, 2048, 256])
    obv_s = dout("obv_s", [4, 2048, 256])

    gin_a = [dscr("gin_a%d" % l, [128, 512], BF16) for l in range(2)]
    gout_a = [dscr("gout_a%d" % l, [NCORE * 128, 512], BF16) for l in range(2)]
    gin_bk = dscr("gin_bk", [128, 2048], BF16)
    gout_bk = dscr("gout_bk", [NCORE * 128, 2048], BF16)
    gin_bv = dscr("gin_bv", [TP, 256], BF16)
    gout_bv = dscr("gout_bv", [NCORE * TP, 256], BF16)
    kbs_scr = dscr("kbs_scr", [128, 64], BF16)
    vbs_scr = dscr("vbs_scr", [8, 4 * 256], BF16)

    es = ExitStack()

    def sb(name, shape, dt):
        return es.enter_context(nc.sbuf_tensor(name, list(shape), dt))

    XBh = sb("XB", [128, NCH * T], BF16)
    ZRh = sb("ZR", [128, NCH * T], F32)
    U1h = sb("U1", [128, NCH * T], BF16)
    WRh = sb("WR", [128, NSLOT * 16 * PW], BF16)
    CSh = sb("CS", [128, 2 * T], F32)
    CTh = sb("CT", [128, 2 * 9 * 32], F32)
    MKh = sb("MK", [128, 8 * 128], BF16)
    LNh = sb("LNP", [128, 2 * 128], F32)
    SKh = sb("SK", [128, 32], F32)
    IDBh = sb("IDB", [128, 128], BF16)
    IDFh = sb("IDF", [128, 128], F32)
    ONEh = sb("ONE", [128, 128], BF16)
    TMh = sb("TM", [128, 6 * 512], F32)
    TBh = sb("TB", [128, 6 * 512], BF16)
    STh = sb("STG", [128, 2 * 512], F32)
    RDh = sb("RD", [128, 256], F32)
    PSh = es.enter_context(nc.psum_tensor("PS", [128, 8 * 512], F32))

    XB = Region("XB", XBh, 2, 0, BF16, (NCH, T))
    Z = Region("ZR", ZRh, 4, 0, F32, (NCH, T))
    OT = Region("U1", U1h, 2, 0, BF16, (NCH, T))
    HID = OT
    XSTG = Region("U1", U1h, 2, 0, BF16, (2, D))
    OSTG = Region("U1", U1h, 2, 0, F32, (2, D))
    WR = Region("WR", WRh, 2, 0, BF16, (NSLOT, 16, PW))
    COSF = Region("CS", CSh, 4, 0, F32, (T,))
    SINF = Region("CS", CSh, 4, T * 4, F32, (T,))
    COST = Region("CT", CTh, 4, 0, F32, (9, 32))
    SINT = Region("CT", CTh, 4, 9 * 32 * 4, F32, (9, 32))
    MK = Region("MK", MKh, 2, 0, BF16, (8, 128))
    LNG = Region("LNP", LNh, 4, 0, F32, (8, 16))
    LNB = Region("LNP", LNh, 4, 128 * 4, F32, (8, 16))
    SK = Region("SK", SKh, 4, 0, F32, (2, 16))
    IDB = Region("IDB", IDBh, 2, 0, BF16, (128,))
    IDF = Region("IDF", IDFh, 4, 0, F32, (128,))
    ONE = Region("ONE", ONEh, 2, 0, BF16, (128,))
    TM = Region("TM", TMh, 4, 0, F32, (6, 512))
    TB = Region("TB", TBh, 2, 0, BF16, (6, 512))
    STG = Region("STG", STh, 4, 0, F32, (2, 512))
    RD = Region("RD", RDh, 4, 0, F32, (256,))
    PS = Region("PS", PSh, 4, 0, F32, (8, 512))
    PSB = Region("PS", PSh, 4, 0, BF16, (8, 1024))

    def zr(off_f32, dt, shape):
        return Region("ZR", ZRh, 4, off_f32 * 4, dt, shape)

    QT = zr(0, BF16, (16, T))
    KT = zr(8448, BF16, (2, 1184))
    VT = zr(9632, BF16, (9, 256))
    VS = zr(10784, BF16, (4, 256))
    KCT = zr(11296, BF16, (4, 2, 128))
    VC = zr(11808, BF16, (4, 256))
    KTOKC = zr(12320, BF16, (4, 256))
    ACC = zr(0, F32, (2, 2, TP))
    KTB = zr(4096, BF16, (2, 3104))
    VB1 = zr(7200, BF16, (9, 256))
    VB4 = zr(8352, BF16, (4, 3, 256))
    VB16 = zr(9888, BF16, (16, 2, 256))
    QBUF = zr(13984, BF16, (2, 2, T))
    QS = zr(16096, BF16, (3, 8, 2, TS))
    KTOKB = zr(8352, BF16, (13, 256))
    KCTB = zr(8352 + 1664, BF16, (13, 2, 128))
    VCB = zr(8352 + 3328, BF16, (13, 256))
    VSB = zr(7200, BF16, (4, 256))

    def E(eng, fn, reads=(), writes=(), dma=False, coll=False):
        return tr.add(eng, fn, reads, writes, dma, coll)

    def dram_res(name, lo=0, hi=1 << 40):
        return (name, lo, hi)

    psum_state = {}

    def ps_meta(ap):
        esz = 4 if ap.dtype == F32 else 2
        stride0, npart = ap.ap[0]
        p0 = ap.offset // stride0
        c0 = ap.offset % stride0
        ext = 0
        for (st_, cn_) in ap.ap[1:]:
            ext += (cn_ - 1) * abs(st_)
        b0 = c0 * esz
        b1 = (c0 + ext + 1) * esz
        bank = b0 // 2048
        assert (b1 - 1) // 2048 == bank
        quads = tuple(range(p0 // 32, (p0 + npart - 1) // 32 + 1))
        return bank, quads, b0, b1

    def start_flag(ap, first, last):
        bank, quads, b0, b1 = ps_meta(ap)
        key = (b0, b1)
        use_start = False
        if first:
            any_open = any(psum_state.setdefault((bank, q), {"open": set(), "wr": []})["open"] for q in quads)
            if not any_open:
                use_start = True
                for q in quads:
                    st_ = psum_state[(bank, q)]
                    st_["wr"] = []
            else:
                for q in quads:
                    for (w0, w1) in psum_state[(bank, q)]["wr"]:
                        assert not (w0 < b1 and b0 < w1), "psum group start on dirty columns"
            for q in quads:
                st_ = psum_state[(bank, q)]
                st_["open"].add(key)
                st_["wr"].append(key)
        if last:
            for q in quads:
                st_ = psum_state.setdefault((bank, q), {"open": set(), "wr": []})
                st_["open"].discard(key)
        return use_start

    def mm(out, lhsT, rhs, start, stop, tp=None):
        (oa, orr), (la, lr), (ra, rr) = out, lhsT, rhs
        start = start_flag(oa, start, stop)
        if tp is None:
            E("pe", lambda e: e.matmul(oa, la, ra, start=start, stop=stop, skip_group_check=True), [lr, rr], [orr])
        else:
            E("pe", lambda e: e.matmul(oa, la, ra, start=start, stop=stop, tile_position=tp, skip_group_check=True),
              [lr, rr], [orr])

    def transpose(out, in_, ident):
        (oa, orr), (ia, ir), (da, drr) = out, in_, ident
        E("pe", lambda e: e.transpose(oa, ia, da), [ir, drr], [orr])

    def act(out, in_, func, bias=None, scale=None):
        (oa, orr), (ia, ir) = out, in_
        reads = [ir]
        kw = {}
        if bias is not None:
            if isinstance(bias, tuple):
                kw["bias"] = bias[0]
                reads.append(bias[1])
            else:
                kw["bias"] = bias
        if scale is not None:
            if isinstance(scale, tuple):
                kw["scale"] = scale[0]
                reads.append(scale[1])
            else:
                kw["scale"] = scale
        E("act", lambda e: e.activation(oa, ia, func, **kw), reads, [orr])

    def tt(eng, out, a, b, op):
        (oa, orr), (aa, ar), (ba, br) = out, a, b
        E(eng, lambda e: e.tensor_tensor(oa, aa, ba, op), [ar, br], [orr])

    def ts(eng, out, a, s1, op0, s2=None, op1=None):
        (oa, orr), (aa, ar) = out, a
        reads = [ar]
        if isinstance(s1, tuple):
            reads.append(s1[1])
            s1 = s1[0]
        if op1 is None:
            E(eng, lambda e: e.tensor_scalar(oa, aa, s1, None, op0), reads, [orr])
        else:
            E(eng, lambda e: e.tensor_scalar(oa, aa, s1, s2, op0, op1), reads, [orr])

    def stt(eng, out, a, scalar, b, op0, op1):
        (oa, orr), (aa, ar), (ba, br) = out, a, b
        E(eng, lambda e: e.scalar_tensor_tensor(oa, aa, scalar, ba, op0, op1), [ar, br], [orr])

    def cp(eng, out, in_):
        (oa, orr), (ia, ir) = out, in_
        if eng == "act":
            E("act", lambda e: e.activation(oa, ia, AF.Copy), [ir], [orr])
        else:
            E(eng, lambda e: e.tensor_copy(oa, ia), [ir], [orr])

    def recip(out, in_):
        (oa, orr), (ia, ir) = out, in_
        E("dve", lambda e: e.reciprocal(oa, ia), [ir], [orr])

    def dma(q, out, in_, reads, writes):
        E(q, lambda e: e.dma_start(out=out, in_=in_), reads, writes, dma=True)

    def gather(g_in, g_out, rows):
        if fake_gather:
            for r in range(NCORE):
                dma("pool", g_out[r * rows:(r + 1) * rows, :], g_in[:, :], [dram_res(g_in.name)], [dram_res(g_out.name)])
        else:
            op = E("pool", lambda e: e.collective_compute("AllGather", ALU.bypass, replica_groups=[list(range(NCORE))],
                                                          ins=[g_in.ap().opt()], outs=[g_out.ap().opt()]),
                   [dram_res(g_in.name)], [dram_res(g_out.name)], coll=True)
            for prev in tr.dma_last.values():
                if prev is not op:
                    op.deps.add(prev)

    dma("sp", CSh[:, 0:T], cosf[:, :], [], [COSF()[1]])
    dma("sp", CSh[:, T:2 * T], sinf[:, :], [], [SINF()[1]])
    dma("sp", CTh[:, 0:288], cost[:, :], [], [COST()[1]])
    dma("sp", CTh[:, 288:576], sint[:, :], [], [SINT()[1]])
    dma("sp", MKh[:, :], masks[:, :], [], [MK()[1]])
    dma("sp", LNh[:, 0:128], lng[:, :], [], [LNG()[1]])
    dma("sp", LNh[:, 128:256], lnb[:, :], [], [LNB()[1]])
    dma("sp", SKh[:, :], sinks[:, :], [], [SK()[1]])
    dma("sp", IDBh[:, :], identb[:, :], [], [IDB()[1]])
    dma("sp", IDFh[:, :], identf[:, :], [], [IDF()[1]])
    E("dve", lambda e: e.memset(ONEh[:, :], 1.0), [], [ONE()[1]])
    act(SK(), SK(), AF.Exp)

    panels = []

    def wview(h, lead, k0, c0, ncols=PW):
        if lead is not None:
            ap = h[lead, k0:k0 + D, c0:c0 + ncols]
        else:
            ap = h[k0:k0 + D, c0:c0 + ncols]
        return ap.rearrange("(kc p) n -> p kc n", p=128)

    for l in range(2):
        for i in range(9):
            panels.append(wview(wqa, l, 0, i * PW))
        panels.append(wview(wkva, l, 0, 0))
        panels.append(wview(wkva, l, 0, 256))
        for i in range(8):
            panels.append(wview(woa, l, 0, i * PW))
        if small == 1:
            break
        if small == 2:
            continue
        for hg in range(4):
            for i in range(8):
                panels.append(wview(wup, l, 0, hg * D + i * PW))
            for i in range(8):
                panels.append(wview(wdn, l, hg * D, i * PW))
    for l in range(2, 4):
        if small == 1:
            break
        jb = l - 2
        if l == 2:
            for i in range(3):
                panels.append(wview(wkvb, None, 0, i * PW))
        for i in range(8):
            for g in range(3):
                panels.append(wview(wqb, jb, 0, g * D + i * PW))
        for i in range(8):
            panels.append(wview(wob, jb, 0, i * PW))
        if small == 2:
            continue
        for hg in range(4):
            for i in range(8):
                panels.append(wview(wup, l, 0, hg * D + i * PW))
            for i in range(8):
                panels.append(wview(wdn, l, hg * D, i * PW))

    pstate = {"issued": 0, "next": 0}

    def issue_panels(held=0):
        while pstate["issued"] < len(panels) and pstate["issued"] < pstate["next"] + NSLOT - held:
            p = pstate["issued"]
            slot = p % NSLOT
            ap, res = WR(slot)
            src = panels[p]
            dma("pool", sq(ap, 0), src, [], [res])
            pstate["issued"] += 1

    def next_panel(held=0):
        issue_panels(held)
        p = pstate["next"]
        pstate["next"] += 1
        return p % NSLOT

    def Wl(slot, kc, c0, n):
        ap, res = WR(slot, kc, (c0, n))
        return sq(ap, 0, 1), res

    rot = {"i": 0}

    def next_bank(nb=4):
        b = rot["i"] % nb
        rot["i"] += 1
        return b

    def xb_cols(kc, c0, n):
        ap, res = XB(kc, (c0, n))
        return sq(ap, 0), res

    def ps_bank(b, n, p=None, c0=0):
        ap, res = PS(b, (c0, n), p=p)
        return sq(ap, 0), res

    def tm(i, n, p=None):
        ap, res = TM(i, (0, n), p=p)
        return sq(ap, 0), res

    def tb(i, n, p=None, c0=0):
        ap, res = TB(i, (c0, n), p=p)
        return sq(ap, 0), res

    TOKT = [(i * 128, 128) for i in range(8)] + [(1024, 32)]
    for ti, (r0, nt) in enumerate(TOKT):
        sbuf_i = ti % 2
        sap, sres = XSTG(sbuf_i, p=(0, nt))
        dma("pool", sq(sap, 0), x_in[r0:r0 + nt, :], [], [sres])
        for hb in range(2):
            bank = 4 + hb
            for k in range(8):
                kc = hb * 8 + k
                oap, ores = PSB(bank, (k * 128, nt))
                iap, ires = XSTG(sbuf_i, (kc * 128, 128), p=(0, nt))
                dap, dres = IDB((0, nt), p=(0, nt))
                transpose((sq(oap, 0), ores), (sq(iap, 0), ires), (dap, dres))
            oap, ores = PSB(bank, (0, 1024))
            src = sq(oap, 0).rearrange("p (k n) -> p k n", k=8)[:, :, 0:nt]
            dap, dres = XB((hb * 8, 8), (r0, nt))
            cp("act" if hb == 0 else "dve", (dap, dres), (src, ores))

    def rope_pair(bA, bB, c0, n, dst1, dst2):
        A = tm(0, n)
        B = tm(1, n)
        cp("act", A, ps_bank(bA, n))
        cp("act", B, ps_bank(bB, n))
        cs = (COSF((c0, n))[0], COSF((c0, n))[1])
        sn = (SINF((c0, n))[0], SINF((c0, n))[1])
        t1 = tm(2, n)
        t2 = tm(3, n)
        tt("dve", t1, A, cs, ALU.mult)
        tt("dve", t2, B, sn, ALU.mult)
        tt("dve", dst1, t1, t2, ALU.subtract)
        tt("dve", t1, B, cs, ALU.mult)
        tt("dve", t2, A, sn, ALU.mult)
        tt("dve", dst2, t1, t2, ALU.add)

    def proj_pair(slot, dstfn, fixed=False):
        for (c0, n) in TTS:
            bsel = 0 if fixed else next_bank(2)
            bA, bB = 2 * bsel, 2 * bsel + 1
            for oc, bk in ((0, bA), (1, bB)):
                for kc in range(16):
                    mm(ps_bank(bk, n), Wl(slot, kc, oc * 128, 128), xb_cols(kc, c0, n), kc == 0, kc == 15)
            rope_pair(bA, bB, c0, n, dstfn(0, c0, n), dstfn(1, c0, n))

    def rope_tok(psK, nt, tidx, dst):
        ka_, kr_ = TM(4, (0, 256), p=(0, nt))
        ks = (sq(ka_, 0), kr_)
        cp("act", ks, psK)
        cap, cres = COST(tidx, p=(0, nt))
        sap, sres = SINT(tidx, p=(0, nt))
        cs_ = (sq(cap, 0), cres)
        sn_ = (sq(sap, 0), sres)
        dap, dres = dst
        t1a, t1r = TM(2, (0, 32), p=(0, nt))
        t2a, t2r = TM(3, (0, 32), p=(0, nt))
        t1 = (sq(t1a, 0), t1r)
        t2 = (sq(t2a, 0), t2r)
        for h in range(4):
            x1 = (ks[0][:, h * 64:h * 64 + 32], kr_)
            x2 = (ks[0][:, h * 64 + 32:h * 64 + 64], kr_)
            d1 = (dap[:, h * 64:h * 64 + 32], dres)
            d2 = (dap[:, h * 64 + 32:h * 64 + 64], dres)
            tt("dve", t1, x1, cs_, ALU.mult)
            tt("dve", t2, x2, sn_, ALU.mult)
            tt("dve", d1, t1, t2, ALU.subtract)
            tt("dve", t1, x2, cs_, ALU.mult)
            tt("dve", t2, x1, sn_, ALU.mult)
            tt("dve", d2, t1, t2, ALU.add)

    def tok_proj(slot, c0, nt, bank, col0):
        for kc in range(16):
            mm(ps_bank(bank, PW, p=(0, nt), c0=col0), xb_cols(kc, c0, nt), Wl(slot, kc, 0, PW), kc == 0, kc == 15)

    arot = {"st": 0, "ud": 0, "pt": 0}

    apipe = {"pending": None}

    def attn_flush():
        if apipe["pending"] is not None:
            f = apipe["pending"]
            apipe["pending"] = None
            f()

    def attn_kb(nk, W, qfn, ktfn, vfn, mask, nq, npairs, udbank, first, last, ucols=None, post=None):
        for half in range(2):
            for j in range(4):
                oap, ores = PS(2 + j, (0, W), p=(0, nk))
                oap = sq(oap, 0)
                if npairs > 1:
                    oap = oap.rearrange("p (i t) -> p i t", i=npairs)
                qa, qr = qfn(j, half)
                ka, kr = ktfn(j, half)
                mm((oap, ores), (ka, kr), (qa, qr), half == 0, half == 1, tp=(32 * j, 0) if j == 3 else None)
        pti = 4 + (arot["pt"] % 2)
        arot["pt"] += 1
        pt = tb(pti, 4 * W, p=(0, nk))
        sap, sres = PS((2, 4), (0, W), p=(0, nk))
        act((pt[0].rearrange("p (j w) -> p j w", j=4), pt[1]), (sap, sres), AF.Exp, scale=0.125)
        ma, mr = mask
        g = 4 * npairs
        pv = pt[0].rearrange("p (g q) -> p g q", q=nq)
        tt("dve", (pv, pt[1]), (pv, pt[1]), (ma.unsqueeze(1).broadcast_to([nk, g, nq]), mr), ALU.mult)
        prev = apipe["pending"]
        apipe["pending"] = lambda: pv_part(nk, W, vfn, npairs, udbank, first, last, ucols, pti, post)
        if prev is not None:
            prev()

    def pv_part(nk, W, vfn, npairs, udbank, first, last, ucols, pti, post):
        for j in range(4):
            a, hf = j // 2, j % 2
            pj = tb(pti, W, p=(0, nk), c0=j * W)
            if ucols is not None and npairs > 1:
                pj = (pj[0].rearrange("p (i t) -> p i t", i=npairs), pj[1])
            va, vr = vfn(j)
            if ucols is None:
                uo = PS(udbank, ((0 * 2 + a) * W, W), p=(hf * 64, 64))
                do = PS(udbank, ((1 * 2 + a) * W, W), p=(hf * 64, 64))
                uo = (sq(uo[0], 0), uo[1])
                do = (sq(do[0], 0), do[1])
            else:
                uo = ucols(0, a, hf)
                do = ucols(1, a, hf)
            mm(uo, (va, vr), pj, first, last)
            oa, orr = ONE((0, 64), p=(0, nk))
            mm(do, (oa, orr), pj, first, last)
        if post is not None:
            post()

    def mask_ap(mi, nk, nq):
        ap, res = MK(mi, (0, nq), p=(0, nk))
        return sq(ap, 0), res

    stat_state = {"n": 0, "pending": []}

    def epilogue(oc, ti, c0, n, bank, ln_stats, first_partial=True):
        zap, zres = Z(oc, (c0, n))
        zz = (sq(zap, 0), zres)
        if first_partial:
            stt("dve", zz, xb_cols(oc, c0, n), ALPHA, ps_bank(bank, n), ALU.mult, ALU.add)
        else:
            tt("dve", zz, zz, ps_bank(bank, n), ALU.add)
        if ln_stats:
            k = stat_state["n"] % 2
            stat_state["n"] += 1
            zb = tb(k, n)
            sqb = tb(2 + k, n)
            cp("act", zb, zz)
            act(sqb, zz, AF.Square)
            stat_state["pending"].append((oc, ti, n, zb, sqb))

    def flush_stats():
        for (oc, ti, n, zb, sqb) in stat_state["pending"]:
            oa, orr = ONE((0, 128))
            mm(ps_bank(2 + ti, n), (oa, orr), zb, oc == 0, oc == 15)
            mm(ps_bank(5 + ti, n), (oa, orr), sqb, oc == 0, oc == 15)
        stat_state["pending"] = []

    def ln_finalize(lni, ti, c0, n, final):
        m = tm(0, n)
        rstd = tm(1, n)
        var = tm(4, n)
        ts("dve", m, ps_bank(2 + ti, n), 1.0 / D, ALU.mult)
        tt("dve", var, m, m, ALU.mult)
        stt("dve", var, ps_bank(5 + ti, n), 1.0 / D, var, ALU.mult, ALU.subtract)
        ts("dve", var, var, EPS, ALU.add)
        act(var, var, AF.Sqrt)
        recip(rstd, var)
        for c in range(16):
            t = tm(2 + (c % 2), n)
            zap, zres = Z(c, (c0, n))
            zz = (sq(zap, 0), zres)
            tt("dve", t, zz, m, ALU.subtract)
            tt("dve", t, t, rstd, ALU.mult)
            ga, gr = LNG(lni, c)
            ba, br = LNB(lni, c)
            g1 = (sq(ga, 0), gr)
            b1 = (sq(ba, 0), br)
            act(xb_cols(c, c0, n), t, AF.Identity, bias=b1, scale=g1)
            if final:
                act(zz, t, AF.Identity, bias=b1, scale=g1)

    def dense_ln(nslots_fn, lni, final=False, kc_src=None):
        for pi in range(8):
            slot = next_panel()
            for ocl in range(2):
                oc = pi * 2 + ocl
                for ti, (c0, n) in enumerate(TTS):
                    bank = next_bank(2)
                    for kc in range(16):
                        sa, sr = OT(kc, (c0, n))
                        mm(ps_bank(bank, n), Wl(slot, kc, ocl * 128, 128), (sq(sa, 0), sr), kc == 0, kc == 15)
                    flush_stats()
                    epilogue(oc, ti, c0, n, bank, True)
        flush_stats()
        for ti, (c0, n) in enumerate(TTS):
            ln_finalize(lni, ti, c0, n, final)

    def mlp(l, final):
        lni = 2 * l + 1
        if small == 2:
            return
        for hg in range(4):
            for pi in range(8):
                slot = next_panel()
                for ocl in range(2):
                    hc = pi * 2 + ocl
                    for ti, (c0, n) in enumerate(TTS):
                        bank = next_bank(4)
                        for kc in range(16):
                            mm(ps_bank(bank, n), Wl(slot, kc, ocl * 128, 128), xb_cols(kc, c0, n), kc == 0, kc == 15)
                        r = tm(2 + (rot["i"] % 2), n)
                        act(r, ps_bank(bank, n), AF.Relu)
                        ha, hr = HID(hc, (c0, n))
                        tt("dve", (sq(ha, 0), hr), r, r, ALU.mult)
            for pi in range(8):
                slot = next_panel()
                for ocl in range(2):
                    oc = pi * 2 + ocl
                    for ti, (c0, n) in enumerate(TTS):
                        bank = next_bank(2 if hg == 3 else 4)
                        for kc in range(16):
                            ha, hr = HID(kc, (c0, n))
                            mm(ps_bank(bank, n), Wl(slot, kc, ocl * 128, 128), (sq(ha, 0), hr), kc == 0, kc == 15)
                        flush_stats()
                        epilogue(oc, ti, c0, n, bank, hg == 3, first_partial=(hg == 0))
        flush_stats()
        for ti, (c0, n) in enumerate(TTS):
            ln_finalize(lni, ti, c0, n, final)

    cid = {}

    def core_vals(e):
        if "pid" not in cid:
            pid = e.partition_id()
            cid["pid"] = pid
            cid["m1"] = (pid + 7) % 8
            cid["m2"] = (pid + 6) % 8
        return cid

    def layer_a(l):
        checkpoint(101 + 1000 * l)
        for i in range(8):
            slot = next_panel()

            def dq(half, c0, n, i=i):
                ap, res = QT(2 * i + half, (c0, n))
                return sq(ap, 0), res
            proj_pair(slot, dq)
            checkpoint(102 + 1000 * l)
        slot = next_panel()

        def dk(half, c0, n):
            ap, res = KT(half, (128 + c0, n))
            return sq(ap, 0), res
        proj_pair(slot, dk)
        checkpoint(103 + 1000 * l)
        slotK = next_panel()
        slotV = next_panel(held=1)
        for ti in range(8):
            c0 = ti * 128
            bank = next_bank(4)
            if ti == 7:
                tok_proj(slotK, c0, 128, bank, 0)
            tok_proj(slotV, c0, 128, bank, 256)
            va, vr = VT(ti + 1)
            cp("act", (sq(va, 0), vr), ps_bank(bank, 256, c0=256))
            if ti == 7:
                st = STG(0)
                sa = sq(st[0], 0)
                rope_tok(ps_bank(bank, 256), 128, 7, (sa[:, 0:256], st[1]))
                cp("act", (sa[:, 256:512], st[1]), ps_bank(bank, 256, c0=256))
                dma("sp", oak_p[l, :, :], sa[:, 0:256], [st[1]], [dram_res("oak_p")])
                dma("sp", oav_p[l, :, :], sa[:, 256:512], [st[1]], [dram_res("oav_p")])
        checkpoint(105 + 1000 * l)
        for s in range(4):
            c0 = TP + 8 * s
            bank = next_bank(4)
            tok_proj(slotK, c0, 8, bank, 0)
            tok_proj(slotV, c0, 8, bank, 256)
            va, vr = VS(s, p=(0, 8))
            cp("act", (sq(va, 0), vr), ps_bank(bank, 256, p=(0, 8), c0=256))
            st = STG(1, p=(0, 8))
            sa = sq(st[0], 0)
            rope_tok(ps_bank(bank, 256, p=(0, 8)), 8, 8, (sa[:, 0:256], st[1]))
            cp("act", (sa[:, 256:512], st[1]), ps_bank(bank, 256, p=(0, 8), c0=256))
            dma("sp", oak_s[l, s, 120:128, :], sa[:, 0:256], [st[1]], [dram_res("oak_s")])
            dma("sp", oav_s[l, s, 120:128, :], sa[:, 256:512], [st[1]], [dram_res("oav_s")])
            dma("sp", oak_s[l, s, 0:120, :], cak[l, s, 8:128, :], [], [dram_res("oak_s")])
            dma("sp", oav_s[l, s, 0:120, :], cav[l, s, 8:128, :], [], [dram_res("oav_s")])
        checkpoint(1 + 10 * l)
        g_in, g_out = gin_a[l], gout_a[l]
        ka, kr = KT((0, 2), (128 + 896, 128))
        dma("sp", g_in[:, 0:256].rearrange("p (h n) -> p h n", h=2), ka, [kr], [dram_res(g_in.name)])
        va, vr = VT(8)
        dma("sp", g_in[:, 256:512], sq(va, 0), [vr], [dram_res(g_in.name)])
        gather(g_in, g_out, 128)
        ka0, kr0 = KT((0, 2), (0, 128))
        va0, vr0 = VT(0)

        def halo_k(e):
            cv = core_vals(e)
            return e.dma_start(out=ka0, in_=g_out[bass.ds(cv["m1"] * 128, 128), 0:256].rearrange("p (h n) -> p h n", h=2))

        def halo_v(e):
            cv = core_vals(e)
            return e.dma_start(out=sq(va0, 0), in_=g_out[bass.ds(cv["m1"] * 128, 128), 256:512])
        E("pool", halo_k, [dram_res(g_out.name)], [kr0], dma=True)
        E("pool", halo_v, [dram_res(g_out.name)], [vr0], dma=True)
        checkpoint(110 + 1000 * l)
        for s in range(4):
            ka, kr = KTOKC(s)
            for half in range(2):
                dma("pool", sq(ka, 0)[:, half * 128:(half + 1) * 128].rearrange("p (h d) -> p h d", h=4),
                    cak[l, s, :, :].rearrange("m (h t d) -> m h t d", h=4, t=2)[:, :, half, :], [], [kr])
            va, vr = VC(s)
            dma("pool", sq(va, 0), cav[l, s, :, :], [], [vr])
            bank = 4 + (s % 2)
            for half in range(2):
                src = sq(ka, 0)[:, half * 128:(half + 1) * 128]
                oap, ores = PSB(bank, (half * 128, 128))
                transpose((sq(oap, 0), ores), (src, kr), IDB((0, 128)))
            oap, ores = PSB(bank, (0, 256))
            da, drr = KCT(s)
            cp("act", (sq(da, 0), drr), (sq(oap, 0).rearrange("p (h n) -> p h n", h=2), ores))
        checkpoint(111 + 1000 * l)
        for b in [1, 2, 3, 4, 5, 6, 7, 0]:
            for i in range(8):
                udb = 6 + (arot["ud"] % 2)
                arot["ud"] += 1

                def qfn(j, half, i=i, b=b):
                    ap, res = QT(2 * i + half, (128 * b, 128), p=(32 * j, 32))
                    return sq(ap, 0), res
                for kb in range(2):
                    def ktfn(j, half, b=b, kb=kb):
                        ap, res = KT(half, (128 * (b + kb), 128), p=(32 * j, 32))
                        return sq(ap, 0), res

                    def vfn(j, b=b, kb=kb):
                        ap, res = VT(b + kb, (64 * j, 64))
                        return sq(ap, 0), res
                    mi = (3 if b == 0 else 1) if kb == 0 else 0

                    def post(i=i, b=b, udb=udb):
                        W = 128
                        ud = tm(5, 4 * W)
                        cp("act", ud, ps_bank(udb, 4 * W))
                        rd = RD((0, 2 * W))
                        for a in range(2):
                            sa_, sr_ = SK(l, 2 * i + a)
                            ra_, rr_ = RD((a * W, W))
                            ts("dve", (ra_, rr_), (ud[0][:, (2 + a) * W:(3 + a) * W], ud[1]), (sq(sa_, 0), sr_), ALU.add)
                        recip(rd, rd)
                        for a in range(2):
                            oa, orr = OT(2 * i + a, (128 * b, 128))
                            ra_, rr_ = RD((a * W, W))
                            tt("dve", (sq(oa, 0), orr), (ud[0][:, a * W:(a + 1) * W], ud[1]), (ra_, rr_), ALU.mult)
                    attn_kb(128, 128, qfn, ktfn, vfn, mask_ap(mi, 128, 128), 128, 1, udb, kb == 0, kb == 1,
                            post=post if kb == 1 else None)
        attn_flush()
        checkpoint(113 + 1000 * l)
        for s in range(4):
            udb = 6 + (arot["ud"] % 2)
            arot["ud"] += 1
            W = 64

            def qfn(j, half, s=s):
                ap, res = QT((half, 8, 2), (TP + 8 * s, 8), p=(32 * j, 32))
                return ap, res
            for kb in range(2):
                if kb == 0:
                    def ktfn(j, half, s=s):
                        ap, res = KCT(s, half, p=(32 * j, 32))
                        return sq(ap, 0, 1), res

                    def vfn(j, s=s):
                        ap, res = VC(s, (64 * j, 64))
                        return sq(ap, 0), res
                    nk, mi = 128, 1
                else:
                    def ktfn(j, half, s=s):
                        ap, res = KT(half, (128 + TP + 8 * s, 8), p=(32 * j, 32))
                        return sq(ap, 0), res

                    def vfn(j, s=s):
                        ap, res = VS(s, (64 * j, 64), p=(0, 8))
                        return sq(ap, 0), res
                    nk, mi = 8, 0
                def post(s=s, udb=udb, W=W):
                    ud = tm(5, 4 * W)
                    cp("act", ud, ps_bank(udb, 4 * W))
                    rd = RD((0, 2 * W))
                    for a in range(2):
                        for i in range(8):
                            sa_, sr_ = SK(l, 2 * i + a)
                            ra_, rr_ = RD((a * W + 8 * i, 8))
                            ts("dve", (ra_, rr_), (ud[0][:, (2 + a) * W + 8 * i:(2 + a) * W + 8 * i + 8], ud[1]),
                               (sq(sa_, 0), sr_), ALU.add)
                    recip(rd, rd)
                    for a in range(2):
                        oa, orr = OT((a, 8, 2), (TP + 8 * s, 8))
                        ra_, rr_ = RD((a * W, W))
                        tt("dve", (oa, orr), (ud[0][:, a * W:(a + 1) * W].rearrange("p (i t) -> p i t", i=8), ud[1]),
                           (ra_.rearrange("p (i t) -> p i t", i=8), rr_), ALU.mult)
                attn_kb(nk, W, qfn, ktfn, vfn, mask_ap(mi, nk, 8), 8, 8, udb, kb == 0, kb == 1,
                        post=post if kb == 1 else None)
        attn_flush()
        checkpoint(2 + 10 * l)
        dense_ln(None, 2 * l)
        checkpoint(3 + 10 * l)
        mlp(l, False)
        checkpoint(4 + 10 * l)

    def kv_b():
        slot = next_panel()

        def dk(half, c0, n):
            ap, res = KTB(half, (2048 + c0, n))
            return sq(ap, 0), res
        proj_pair(slot, dk)
        slotK = next_panel()
        slotV = next_panel(held=1)
        for ti in range(8):
            c0 = ti * 128
            bank = next_bank(4)
            tok_proj(slotK, c0, 128, bank, 0)
            tok_proj(slotV, c0, 128, bank, 256)
            st = STG(ti % 2)
            sa = sq(st[0], 0)
            rope_tok(ps_bank(bank, 256), 128, ti, (sa[:, 0:256], st[1]))
            cp("act", (sa[:, 256:512], st[1]), ps_bank(bank, 256, c0=256))
            dma("sp", obk_p[c0:c0 + 128, :], sa[:, 0:256], [st[1]], [dram_res("obk_p")])
            dma("sp", obv_p[c0:c0 + 128, :], sa[:, 256:512], [st[1]], [dram_res("obv_p")])
            dma("pool", gin_bv[c0:c0 + 128, :], sa[:, 256:512], [st[1]], [dram_res("gin_bv")])
        for s in range(4):
            c0 = TP + 8 * s
            bank = next_bank(4)
            tok_proj(slotK, c0, 8, bank, 0)
            tok_proj(slotV, c0, 8, bank, 256)
            st = STG(s % 2, p=(0, 8))
            sa = sq(st[0], 0)
            rope_tok(ps_bank(bank, 256, p=(0, 8)), 8, 8, (sa[:, 0:256], st[1]))
            cp("act", (sa[:, 256:512], st[1]), ps_bank(bank, 256, p=(0, 8), c0=256))
            dma("sp", obk_s[s, 2040:2048, :], sa[:, 0:256], [st[1]], [dram_res("obk_s")])
            dma("sp", obv_s[s, 2040:2048, :], sa[:, 256:512], [st[1]], [dram_res("obv_s")])
            dma("pool", vbs_scr[:, 256 * s:256 * s + 256], sa[:, 256:512], [st[1]], [dram_res("vbs_scr")])
            for q4 in range(4):
                r0 = 8 + 510 * q4
                dma("sp", obk_s[s, r0 - 8:r0 + 502, :], cbk[s, r0:r0 + 510, :], [], [dram_res("obk_s")])
                dma("sp", obv_s[s, r0 - 8:r0 + 502, :], cbv[s, r0:r0 + 510, :], [], [dram_res("obv_s")])
        ka, kr = KTB((0, 2), (2048, TP))
        dma("sp", gin_bk[:, :].rearrange("p (h n) -> p h n", h=2), ka, [kr], [dram_res("gin_bk")])
        ka, kr = KTB((0, 2), (3072, TS))
        dma("sp", kbs_scr[:, :].rearrange("p (h n) -> p h n", h=2), ka, [kr], [dram_res("kbs_scr")])
        gather(gin_bk, gout_bk, 128)
        gather(gin_bv, gout_bv, TP)

    def load_kv_b():
        def dyn(out_ap, res, gbuf, which, rows_per, r0, nrows, rearr=None, **kw):
            def f(e):
                cv = core_vals(e)
                src = gbuf[bass.ds(cv[which] * rows_per + r0, nrows), :]
                if rearr is not None:
                    src = src.rearrange(rearr, **kw)
                return e.dma_start(out=out_ap, in_=src)
            E("pool", f, [dram_res(gbuf.name)], [res], dma=True)
        for ci, which in enumerate(("m2", "m1", "pid")):
            ka, kr = KTB((0, 2), (1024 * ci, 1024))
            dyn(ka, kr, gout_bk, which, 128, 0, 128, "p (h n) -> p h n", h=2)
        ka, kr = KTB((0, 2), (3072, TS))
        dma("pool", ka, kbs_scr[:, :].rearrange("p (h n) -> p h n", h=2), [dram_res("kbs_scr")], [kr])
        va, vr = VB1(0)
        dyn(sq(va, 0), vr, gout_bv, "m1", 1024, 896, 128)
        va, vr = VB1((1, 8))
        dyn(va, vr, gout_bv, "pid", 1024, 0, 1024, "(b p) f -> p b f", p=128)
        for blk, (which, r0) in enumerate((("m1", 512), ("pid", 0), ("pid", 512))):
            va, vr = VB4((0, 4), blk)
            dyn(sq(va, 1), vr, gout_bv, which, 1024, r0, 512, "(m r) f -> m r f", r=4)
        va, vr = VB16((0, 16), 0, p=(0, 64))
        dyn(sq(va, 1), vr, gout_bv, "m2", 1024, 0, 1024, "(m r) f -> m r f", r=16)
        va, vr = VB16((0, 16), 0, p=(64, 64))
        dyn(sq(va, 1), vr, gout_bv, "m1", 1024, 0, 1024, "(m r) f -> m r f", r=16)
        va, vr = VB16((0, 16), 1, p=(0, 64))
        dyn(sq(va, 1), vr, gout_bv, "pid", 1024, 0, 1024, "(m r) f -> m r f", r=16)

    def layer_b(l):
        if l == 2:
            kv_b()
        load_kv_b()
        blocks = {0: [], 1: [], 2: []}
        for b in range(8):
            kbs = []
            for kb in range(2):
                def vfn(j, b=b, kb=kb):
                    ap, res = VB1(b + kb, (64 * j, 64))
                    return sq(ap, 0), res
                mi = (4 if b == 0 else 2) if kb == 0 else 0
                kbs.append((1920 + 128 * (b + kb), 1, 128, vfn, mi))
            blocks[0].append((128, 128 * b, 1, kbs))
        for r in range(4):
            for b in range(2):
                kbs = []
                for kb in range(2):
                    def vfn(j, r=r, b=b, kb=kb):
                        ap, res = VB4(r, b + kb, (64 * j, 64))
                        return sq(ap, 0, 1), res
                    mi = (4 if b == 0 else 2) if kb == 0 else 0
                    kbs.append((1536 + 512 * (b + kb) + r, 4, 128, vfn, mi))
                blocks[1].append((128, 512 * b + r, 4, kbs))
        for r in range(16):
            kbs = []
            for kb in range(2):
                nk = 128 if kb == 0 else 64

                def vfn(j, r=r, kb=kb, nk=nk):
                    ap, res = VB16(r, kb, (64 * j, 64), p=(0, nk))
                    return sq(ap, 0, 1), res
                kbs.append((2048 * kb + r, 16, nk, vfn, 5 if kb == 0 else 0))
            blocks[2].append((64, r, 16, kbs))

        jb = l - 2
        for i in range(8):
            for g in range(3):
                slot = next_panel()
                qb = (i * 3 + g) % 2

                def dq(half, c0, n, qb=qb, g=g, i=i):
                    ap, res = QBUF(qb, half, (c0, n))
                    return sq(ap, 0, 1), res
                proj_pair(slot, dq, fixed=True)
                for half in range(2):
                    qa_, qr_ = QS(g, i, half)
                    sa_, sr_ = QBUF(qb, half, (TP, TS))
                    cp("dve", (sq(qa_, 0, 1, 2), qr_), (sq(sa_, 0, 1), sr_))
                for (nq, q0, qst, kbs) in blocks[g]:
                    udb = 6 + (arot["ud"] % 2)
                    arot["ud"] += 1

                    def qfn(j, half, qb=qb, q0=q0, qst=qst, nq=nq):
                        ap, res = QBUF(qb, half, (q0, nq, qst), p=(32 * j, 32))
                        return sq(ap, 0, 1), res
                    for kbi, (k0, kst, nk, vfn, mi) in enumerate(kbs):
                        def ktfn(j, half, k0=k0, kst=kst, nk=nk):
                            ap, res = KTB(half, (k0, nk, kst), p=(32 * j, 32))
                            return sq(ap, 0), res
                        def post(udb=udb, nq=nq, q0=q0, qst=qst, g=g):
                            ud = tm(5, 4 * nq)
                            cp("act", ud, ps_bank(udb, 4 * nq))
                            src = ud[0].rearrange("p (u a q) -> p u a q", u=2, a=2)
                            aa, ar = ACC((0, 2), (0, 2), (q0, nq, qst))
                            if g == 0:
                                cp("dve", (aa, ar), (src, ud[1]))
                            else:
                                tt("dve", (aa, ar), (aa, ar), (src, ud[1]), ALU.add)
                        attn_kb(nk, nq, qfn, ktfn, vfn, mask_ap(mi, nk, nq), nq, 1, udb, kbi == 0, kbi == 1,
                                post=post if kbi == 1 else None)
            attn_flush()
            da, drr = ACC(1)
            recip((sq(da, 0), drr), (sq(da, 0), drr))
            ua, ur = ACC(0)
            oa, orr = OT((2 * i, 2), (0, TP))
            tt("dve", (oa, orr), (sq(ua, 0), ur), (sq(da, 0), drr), ALU.mult)
        va, vr = VSB((0, 4), p=(0, 8))
        dma("pool", va, vbs_scr[:, :].rearrange("p (s f) -> p s f", s=4), [dram_res("vbs_scr")], [vr])
        for s in range(4):
            a0, r0_ = VCB(0)
            dma("pool", sq(a0, 0), cbv[s, 1920:2048, :], [], [r0_])
            a1, r1_ = VCB((1, 4))
            dma("pool", a1, cbv[s, 1536:2048, :].rearrange("(m r) f -> m r f", r=4), [], [r1_])
            a2, r2_ = VCB((5, 8))
            dma("pool", a2, cbv[s, :, :].rearrange("(m r) f -> m r f", r=16)[:, 0:8, :], [], [r2_])
            rowsel = [(1920, 1)] + [(1536 + r, 4) for r in range(4)] + [(t, 16) for t in range(8)]
            for blk, (rs, rstep) in enumerate(rowsel):
                a0, r0_ = KTOKB(blk)
                for half in range(2):
                    dma("pool", sq(a0, 0)[:, half * 128:(half + 1) * 128].rearrange("p (h d) -> p h d", h=4),
                        cbk[s, rs:rs + 127 * rstep + 1:rstep, :].rearrange("m (h t d) -> m h t d", h=4, t=2)[:, :, half, :],
                        [], [r0_])
            for blk in range(13):
                bank = 4 + (blk % 2)
                ka, kr = KTOKB(blk)
                for half in range(2):
                    src = sq(ka, 0)[:, half * 128:(half + 1) * 128]
                    oap, ores = PSB(bank, (half * 128, 128))
                    transpose((sq(oap, 0), ores), (src, kr), IDB((0, 128)))
                oap, ores = PSB(bank, (0, 256))
                da, drr = KCTB(blk)
                cp("act" if blk % 2 == 0 else "dve", (sq(da, 0), drr),
                   (sq(oap, 0).rearrange("p (h n) -> p h n", h=2), ores))
            udb = 6 + (arot["ud"] % 2)
            arot["ud"] += 1
            sblocks = [(0, 0, 1, 8, 0)] + [(1, r, 4, 2, 1 + r) for r in range(4)] + [(2, t, 1, 1, 5 + t) for t in range(8)]
            cur = [(0, 0), (1, 6), (2, 7)]
            nb_total = len(sblocks) + len(cur)
            cnt = 0

            def ucols_fn(t0, tst, nq):
                def f(ud, a, hf):
                    ap, res = PS(udb, ((ud * 2 + a) * 64, 64), p=(hf * 64, 64))
                    v = sq(ap, 0).rearrange("p (i t) -> p i t", i=8)[:, :, t0:t0 + (nq - 1) * tst + 1:tst]
                    return v, res
                return f
            for (g, t0, tst, nq, blk) in sblocks:
                def qfn(j, half, g=g, t0=t0, tst=tst, nq=nq, s=s):
                    ap, res = QS(g, (0, 8), half, (8 * s + t0, nq, tst), p=(32 * j, 32))
                    return sq(ap, 0, 2), res

                def ktfn(j, half, blk=blk):
                    ap, res = KCTB(blk, half, p=(32 * j, 32))
                    return sq(ap, 0, 1), res

                def vfn(j, blk=blk):
                    ap, res = VCB(blk, (64 * j, 64))
                    return sq(ap, 0), res
                attn_kb(128, 8 * nq, qfn, ktfn, vfn, mask_ap(2, 128, nq), nq, 8, udb, cnt == 0, cnt == nb_total - 1,
                        ucols=ucols_fn(t0, tst, nq))
                cnt += 1
            for (g, mi) in cur:
                def qfn(j, half, g=g, s=s):
                    ap, res = QS(g, (0, 8), half, (8 * s, 8), p=(32 * j, 32))
                    return sq(ap, 0, 2), res

                def ktfn(j, half, s=s):
                    ap, res = KTB(half, (3072 + 8 * s, 8), p=(32 * j, 32))
                    return sq(ap, 0), res

                def vfn(j, s=s):
                    ap, res = VSB(s, (64 * j, 64), p=(0, 8))
                    return sq(ap, 0), res
                attn_kb(8, 64, qfn, ktfn, vfn, mask_ap(mi, 8, 8), 8, 8, udb, False, cnt == nb_total - 1,
                        ucols=ucols_fn(0, 1, 8))
                cnt += 1
            attn_flush()
            W = 64
            ud = tm(5, 4 * W)
            cp("act", ud, ps_bank(udb, 4 * W))
            rd = RD((0, 2 * W))
            recip(rd, (ud[0][:, 2 * W:4 * W], ud[1]))
            for a in range(2):
                oa, orr = OT((a, 8, 2), (TP + 8 * s, 8))
                ra_, rr_ = RD((a * W, W))
                tt("dve", (oa, orr), (ud[0][:, a * W:(a + 1) * W].rearrange("p (i t) -> p i t", i=8), ud[1]),
                   (ra_.rearrange("p (i t) -> p i t", i=8), rr_), ALU.mult)
        checkpoint(2 + 10 * l)
        dense_ln(None, 2 * l)
        checkpoint(3 + 10 * l)
        mlp(l, l == 3)
        checkpoint(4 + 10 * l)

    def main_body():
        checkpoint(100)
        layer_a(0)
        layer_a(1)
        layer_b(2)
        layer_b(3)
        output_phase()

    def output_phase():
        for ti, (r0, nt) in enumerate(TOKT):
            sbi = ti % 2
            for qd in range(4):
                bank = next_bank(4)
                for k in range(4):
                    c = qd * 4 + k
                    zap, zres = Z(c, (r0, nt))
                    oap, ores = PS(bank, (k * 128, 128), p=(0, nt))
                    transpose((sq(oap, 0), ores), (sq(zap, 0), zres), IDF((0, 128)))
                oap, ores = PS(bank, (0, 512), p=(0, nt))
                dap, dres = OSTG(sbi, (qd * 512, 512), p=(0, nt))
                cp("act" if qd % 2 == 0 else "dve", (sq(dap, 0), dres), (sq(oap, 0), ores))
            dap, dres = OSTG(sbi, p=(0, nt))
            dma("sp", y[r0:r0 + nt, :], sq(dap, 0), [dres], [dram_res("y")])

    try:
        main_body()
    except _Stop:
        pass
    tr.finalize()

    sems = {}
    engs = ["pe", "act", "dve", "pool", "sp"]
    nms = {}
    for op in tr.ops:
        if op.ms:
            nms[op.eng] = max(nms.get(op.eng, 0), op.ms)
    for e in engs:
        for k in range((nms.get(e, 0) + SEMCH - 1) // SEMCH + 1):
            sems[("eng", e, k)] = es.enter_context(nc.semaphore("s_%s_%d" % (e, k)))
    for q in ("pool", "sp"):
        for s in range(NDMASEM):
            sems[("dma", (q, s))] = es.enter_context(nc.semaphore("d_%s_%d" % (q, s)))
    for s in range(tr.ncoll):
        sems[("coll", s)] = es.enter_context(nc.semaphore("c_%d" % s))
    block = es.enter_context(nc.Block())
    per_eng = {e: [op for op in tr.ops if op.eng == e] for e in engs}

    def run_queue(eobj, name):
        for op in per_eng[name]:
            for (key, v) in op.waits:
                eobj.wait_ge(sems[key], v)
            inst = op.fn(eobj)
            if op.coll:
                inst.then_inc(sems[op.sem])
            elif op.dma:
                inst.then_inc(sems[op.sem], 16)
            elif op.ms is not None:
                inst.then_inc(sems[("eng", name, (op.ms - 1) // SEMCH)], 1)
        if name == "sp":
            for s, c in tr.dma_count.items():
                eobj.wait_ge(sems[("dma", s)], 16 * c)

    @block.tensor
    def _(e):
        run_queue(e, "pe")

    @block.scalar
    def _(e):
        run_queue(e, "act")

    @block.vector
    def _(e):
        run_queue(e, "dve")

    @block.gpsimd
    def _(e):
        run_queue(e, "pool")

    @block.sync
    def _(e):
        run_queue(e, "sp")

    es.close()
    return nc


_NC_CACHE = {}


def prep_inputs(x_prompt, x_sample, cache_a_k, cache_a_v, cache_b_k, cache_b_v, ln_g, ln_b,
                w_qkv_a, sinks_a, w_o_a, w_kv_b, w_q_b, w_o_b, w_up, w_down):
    f32 = np.float32
    qp = qperm_cols()
    kp = kperm_cols()
    orow = operm_rows()
    x_prompt = np.asarray(x_prompt, f32)
    x_sample = np.asarray(x_sample, f32)
    w_qkv_a = np.asarray(w_qkv_a, f32)
    w_q_b = np.asarray(w_q_b, f32)
    w_kv_b = np.asarray(w_kv_b, f32)
    wqa = np.ascontiguousarray(np.concatenate([w_qkv_a[:, :, :2048][:, :, qp], w_qkv_a[:, :, 2048:2304][:, :, kp]], axis=2))
    wkva = np.ascontiguousarray(w_qkv_a[:, :, 2048:2560])
    woa = np.ascontiguousarray(np.asarray(w_o_a, f32)[:, orow, :])
    wkvb = np.ascontiguousarray(np.concatenate([w_kv_b[:, :256][:, kp], w_kv_b], axis=1))
    wqb = np.ascontiguousarray(np.concatenate([w_q_b[:, :, g * 2048:(g + 1) * 2048][:, :, qp] for g in range(3)], axis=2))
    wob = np.ascontiguousarray(np.asarray(w_o_b, f32)[:, orow, :])
    wup = np.ascontiguousarray(np.asarray(w_up, f32))
    wdn = np.ascontiguousarray(np.asarray(w_down, f32))
    lng = np.ascontiguousarray(np.asarray(ln_g, f32).reshape(8, 16, 128).transpose(2, 0, 1).reshape(128, 128))
    lnb = np.ascontiguousarray(np.asarray(ln_b, f32).reshape(8, 16, 128).transpose(2, 0, 1).reshape(128, 128))
    sk = np.zeros((128, 2, 16), f32)
    sa = np.asarray(sinks_a, f32)
    for i in range(8):
        for a in range(2):
            for hf in range(2):
                sk[hf * 64:(hf + 1) * 64, :, 2 * i + a] = sa[:, 8 * (2 * a + hf) + i][None, :]
    sk = sk.reshape(128, 32)
    identb = np.eye(128, dtype=f32).astype(ml_dtypes.bfloat16)
    identf = np.eye(128, dtype=f32)
    kk = np.arange(128)[:, None]
    qq = np.arange(128)[None, :]
    mcur = (kk <= qq).astype(f32)
    mpa = (kk > qq).astype(f32)
    mpb = (kk >= qq).astype(f32)
    m4 = np.zeros((128, 128), f32)
    m16 = np.zeros((128, 128), f32)
    for k_ in range(8):
        for t_ in range(8):
            if k_ <= t_ and (t_ - k_) % 4 == 0:
                m4[k_, t_] = 1
            if k_ == t_:
                m16[k_, t_] = 1
    cak = np.asarray(cache_a_k, f32).reshape(2, 32, 128, 256)
    cav = np.asarray(cache_a_v, f32).reshape(2, 32, 128, 256)
    cbk = np.asarray(cache_b_k, f32).reshape(32, 2048, 256)
    cbv = np.asarray(cache_b_v, f32).reshape(32, 2048, 256)

    in_maps = []
    for c in range(NCORE):
        b, ch = c // 4, c % 4
        c0 = ch * TP
        xin = np.concatenate([x_prompt[b, c0:c0 + TP], x_sample[4 * c:4 * c + 4].reshape(TS, D)], axis=0)
        pos = np.concatenate([np.arange(c0, c0 + TP), PAST + np.tile(np.arange(8), 4)]).astype(np.int64)
        cs, sn = rope_tables(pos)
        cosf = np.tile(cs.T, (4, 1))
        sinf = np.tile(sn.T, (4, 1))
        cost = np.zeros((128, 9, 32), f32)
        sint = np.zeros((128, 9, 32), f32)
        for ti in range(8):
            cost[:, ti] = cs[ti * 128:(ti + 1) * 128]
            sint[:, ti] = sn[ti * 128:(ti + 1) * 128]
        cost[0:8, 8] = cs[TP:TP + 8]
        sint[0:8, 8] = sn[TP:TP + 8]
        v1 = 1.0 if ch >= 1 else 0.0
        v2 = 1.0 if ch >= 2 else 0.0
        m5 = mpb * np.where(kk < 64, v2, v1)
        mk = np.stack([mcur, mpa, mpb, mpa * v1, mpb * v1, m5, m4, m16], axis=1).reshape(128, 8 * 128)
        in_maps.append({
            "x_in": np.ascontiguousarray(xin),
            "cak": np.ascontiguousarray(cak[:, 4 * c:4 * c + 4]),
            "cav": np.ascontiguousarray(cav[:, 4 * c:4 * c + 4]),
            "cbk": np.ascontiguousarray(cbk[4 * c:4 * c + 4]),
            "cbv": np.ascontiguousarray(cbv[4 * c:4 * c + 4]),
            "lng": lng, "lnb": lnb, "sinks": sk,
            "wqa": wqa, "wkva": wkva, "woa": woa, "wkvb": wkvb, "wqb": wqb, "wob": wob, "wup": wup, "wdn": wdn,
            "cosf": np.ascontiguousarray(cosf), "sinf": np.ascontiguousarray(sinf),
            "cost": cost.reshape(128, 288), "sint": sint.reshape(128, 288),
            "masks": mk.astype(ml_dtypes.bfloat16), "identb": identb, "identf": identf,
        })
    return in_maps


def kernel(**inputs):
    in_maps = prep_inputs(**inputs)
    if "nc" not in _NC_CACHE:
        _NC_CACHE["nc"] = build_program()
    nc = _NC_CACHE["nc"]
    res = run_bass_kernel_spmd(nc, in_maps, core_ids=list(range(NCORE)))
    return assemble(res.results)


def assemble(R):
    y_prompt = np.stack([np.concatenate([R[4 * b + k]["y"][:TP] for k in range(4)], axis=0) for b in range(2)], axis=0)
    y_sample = np.concatenate([R[c]["y"][TP:].reshape(4, 8, D) for c in range(NCORE)], axis=0)
    nak_p = np.stack([np.stack([R[4 * b + 3]["oak_p"][l] for b in range(2)], 0) for l in range(2)], 0).reshape(2, 2, 128, 4, 64)
    nav_p = np.stack([np.stack([R[4 * b + 3]["oav_p"][l] for b in range(2)], 0) for l in range(2)], 0).reshape(2, 2, 128, 4, 64)
    nak_s = np.concatenate([R[c]["oak_s"] for c in range(NCORE)], axis=1).reshape(2, 32, 128, 4, 64)
    nav_s = np.concatenate([R[c]["oav_s"] for c in range(NCORE)], axis=1).reshape(2, 32, 128, 4, 64)
    nbk_p = np.stack([np.concatenate([R[4 * b + 2]["obk_p"], R[4 * b + 3]["obk_p"]], 0) for b in range(2)], 0).reshape(2, 2048, 4, 64)
    nbv_p = np.stack([np.concatenate([R[4 * b + 2]["obv_p"], R[4 * b + 3]["obv_p"]], 0) for b in range(2)], 0).reshape(2, 2048, 4, 64)
    nbk_s = np.concatenate([R[c]["obk_s"] for c in range(NCORE)], axis=0).reshape(32, 2048, 4, 64)
    nbv_s = np.concatenate([R[c]["obv_s"] for c in range(NCORE)], axis=0).reshape(32, 2048, 4, 64)
    outs = (y_prompt, y_sample, nak_p, nav_p, nak_s, nav_s, nbk_p, nbv_p, nbk_s, nbv_s)
    return tuple(np.ascontiguousarray(o, dtype=np.float32) for o in outs)
```
